# Optimizing a Trainium2 kernel written in Bass

```python
import math
import jax, jax.numpy as jnp
from jax import lax
import numpy as np

D_MODEL = 1024
BATCH = 16
SEQ = 256
DEPTH = 2
DEC_BATCH = 4
DEC_SEQ = 1024
PAST_LEN = 512

GRID_W = 64
EPS = 1e-6
ROPE_THETA = 10000.0
ROPE_DIM = 32
H_A = 4
DK_A = 32
DV_A = 64
GLA_LR = 16
GLA_TAU = 16.0
GLA_CHUNK = 64
H_B = 8
DN_B = 64
DR_B = ROPE_DIM
DV_B = 64
Q_LORA = 256
KV_LORA = 128
H_C = 4
DC = ROPE_DIM
W_A = H_A * DV_A
W_B = H_B * DV_B
W_C = H_C * 2 * DC
D_MIX = W_A + W_B + W_C
IN_SIZES = (H_A * DK_A, H_A * DK_A, W_A, 2 * GLA_LR, W_A,
            Q_LORA, KV_LORA, DR_B, W_B,
            H_C * 2 * DC, H_C * 2 * DC, W_C, W_C)
N_IN = sum(IN_SIZES)
Q_BLOCK = 128
DENSE_KEY_LIMIT = 2048

kernel_name = "hybrid_gla_mla_diff_diffusion_step"


def rms_norm(x, g):
    x32 = x.astype(jnp.float32)
    y = x32 * lax.rsqrt(jnp.mean(x32 * x32, axis=-1, keepdims=True) + EPS)
    return (y * g.astype(jnp.float32)).astype(x.dtype)


def split_cols(t, sizes):
    idx, acc = [], 0
    for s in sizes[:-1]:
        acc += s
        idx.append(acc)
    return jnp.split(t, idx, axis=-1)


def to_heads(t, h):
    b, n, w = t.shape
    return t.reshape(b, n, h, w // h).transpose(0, 2, 1, 3)


def from_heads(t):
    b, h, n, d = t.shape
    return t.transpose(0, 2, 1, 3).reshape(b, n, h * d)


def axial_rope_tables(n, rot_dim):
    n_rows = n // GRID_W
    row = jnp.repeat(jnp.arange(n_rows), GRID_W).astype(jnp.float32)
    col = jnp.tile(jnp.arange(GRID_W), n_rows).astype(jnp.float32)
    half = rot_dim // 2
    inv = 1.0 / (ROPE_THETA ** (jnp.arange(0, half, 2, dtype=jnp.float32) / half))
    ar = row[:, None] * inv
    ac = col[:, None] * inv
    ang = jnp.concatenate([ar, ar, ac, ac], axis=-1)
    return jnp.cos(ang), jnp.sin(ang)


def _rot_half(z):
    h = z.shape[-1] // 2
    return jnp.concatenate([-z[..., h:], z[..., :h]], axis=-1)


def apply_axial_rope(x, rope):
    cos, sin = rope
    x32 = x.astype(jnp.float32)
    half = x.shape[-1] // 2
    rot = jnp.concatenate([_rot_half(x32[..., :half]), _rot_half(x32[..., half:])], axis=-1)
    return (x32 * cos + rot * sin).astype(x.dtype)


def attend(q, k, v, scale):
    b, h, sq, d = q.shape
    sk, dv = k.shape[2], v.shape[-1]
    k32, v32 = k.astype(jnp.float32), v.astype(jnp.float32)

    def block(qb):
        s = jnp.einsum('bhqd,bhkd->bhqk', qb.astype(jnp.float32), k32) * scale
        p = jax.nn.softmax(s, axis=-1)
        return jnp.einsum('bhqk,bhkv->bhqv', p, v32)

    if sk >= DENSE_KEY_LIMIT and sq % Q_BLOCK == 0:
        qb = q.reshape(b, h, sq // Q_BLOCK, Q_BLOCK, d).transpose(2, 0, 1, 3, 4)
        out = lax.map(block, qb).transpose(1, 2, 0, 3, 4).reshape(b, h, sq, dv)
    else:
        out = block(q)
    return out.astype(q.dtype)


def gla_chunked(q, k, v, log_a, s0):
    b, h, t, dk = q.shape
    dv = v.shape[-1]
    c = GLA_CHUNK
    n = t // c
    f32 = jnp.float32
    q = q.astype(f32).reshape(b, h, n, c, dk)
    k = k.astype(f32).reshape(b, h, n, c, dk)
    v32 = v.astype(f32).reshape(b, h, n, c, dv)
    bcum = jnp.cumsum(log_a.astype(f32).reshape(b, h, n, c, dk), axis=3)
    diff = bcum[..., :, None, :] - bcum[..., None, :, :]
    mask = jnp.tril(jnp.ones((c, c), dtype=bool))[:, :, None]
    decay = jnp.where(mask, jnp.exp(jnp.minimum(diff, 0.0)), 0.0)
    attn = jnp.einsum('bhnid,bhnjd,bhnijd->bhnij', q, k, decay)
    o_intra = jnp.einsum('bhnij,bhnjv->bhniv', attn, v32)
    b_last = bcum[..., -1:, :]
    k_dec = k * jnp.exp(b_last - bcum)
    chunk_state = jnp.einsum('bhncd,bhncv->bhndv', k_dec, v32)
    chunk_decay = jnp.exp(b_last[..., 0, :])

    def step(s, inp):
        d, u = inp
        return d[..., None] * s + u, s

    s_fin, s_prev = lax.scan(step, s0.astype(f32),
                             (jnp.moveaxis(chunk_decay, 2, 0), jnp.moveaxis(chunk_state, 2, 0)))
    s_prev = jnp.moveaxis(s_prev, 0, 2)
    o_inter = jnp.einsum('bhncd,bhndv->bhncv', q * jnp.exp(bcum), s_prev)
    o = (o_intra + o_inter).reshape(b, h, t, dv)
    return o.astype(v.dtype), s_fin.astype(v.dtype)


def gla_bidir(q, k, v, la_f, la_b, s0_f, s0_b):
    o_f, s_f = gla_chunked(q, k, v, la_f, s0_f)
    fl = lambda z: jnp.flip(z, axis=2)
    o_b, s_b = gla_chunked(fl(q), fl(k), fl(v), fl(la_b), s0_b)
    return o_f + fl(o_b), s_f, s_b


def mixer_sublayer(x, mod, P, layer, ctx, rope):
    bsz, t, _ = x.shape
    shift, scale, gate = jnp.split(mod.astype(x.dtype), 3, axis=-1)
    h = rms_norm(x, P['g_pre']) * (1 + scale) + shift
    proj = h @ P['w_in']
    (gq, gk, gv, ga, gg, cq, ckv, kpe, mg, dq, dk, dvv, dg) = split_cols(proj, IN_SIZES)

    qa = to_heads(gq, H_A) * (DK_A ** -0.5)
    ka = to_heads(gk, H_A)
    va = to_heads(gv, H_A)
    ga_f, ga_b = jnp.split(ga, 2, axis=-1)
    la_f = to_heads(jax.nn.log_sigmoid((ga_f @ P['w_gla_af'] + P['b_gla_af']).astype(jnp.float32)) / GLA_TAU, H_A)
    la_b = to_heads(jax.nn.log_sigmoid((ga_b @ P['w_gla_ab'] + P['b_gla_ab']).astype(jnp.float32)) / GLA_TAU, H_A)
    if ctx is None:
        s0_f = jnp.zeros((bsz, H_A, DK_A, DV_A), x.dtype)
        s0_b = s0_f
    else:
        s0_f, s0_b = ctx['gla'][:, 0], ctx['gla'][:, 1]
    oa, s_f, s_b = gla_bidir(qa, ka, va, la_f, la_b, s0_f, s0_b)
    oa = from_heads(rms_norm(oa, P['g_gla']))

    qb = to_heads(rms_norm(cq, P['g_mla_q']) @ P['w_mla_uq'], H_B)
    q_nope, q_pe = qb[..., :DN_B], qb[..., DN_B:]
    ckv_n = rms_norm(ckv, P['g_mla_kv'])
    kvb = to_heads(ckv_n @ P['w_mla_ukv'], H_B)
    k_nope, v_b = kvb[..., :DN_B], kvb[..., DN_B:]
    k_pe = kpe[:, None]
    if rope is not None:
        q_pe = apply_axial_rope(q_pe, rope)
        k_pe = apply_axial_rope(k_pe, rope)
    q_full = jnp.concatenate([q_nope, q_pe], axis=-1)
    k_full = jnp.concatenate([k_nope, jnp.broadcast_to(k_pe, (bsz, H_B, t, DR_B))], axis=-1)
    if ctx is not None:
        sc = ctx['mla_ckv'].shape[1]
        kvc = to_heads(ctx['mla_ckv'] @ P['w_mla_ukv'], H_B)
        kpe_c = jnp.broadcast_to(ctx['mla_kpe'][:, None], (bsz, H_B, sc, DR_B))
        k_full = jnp.concatenate([k_full, jnp.concatenate([kvc[..., :DN_B], kpe_c], axis=-1)], axis=2)
        v_b = jnp.concatenate([v_b, kvc[..., DN_B:]], axis=2)
    ob = from_heads(attend(q_full, k_full, v_b, (DN_B + DR_B) ** -0.5))

    qc = to_heads(dq, H_C)
    kc = to_heads(dk, H_C)
    vc = to_heads(dvv, H_C)
    q1, q2 = qc[..., :DC], qc[..., DC:]
    if rope is not None:
        q1 = apply_axial_rope(q1, rope)
        q2 = apply_axial_rope(q2, rope)
        kc = jnp.concatenate([apply_axial_rope(kc[..., :DC], rope),
                              apply_axial_rope(kc[..., DC:], rope)], axis=-1)
    k_att, v_att = kc, vc
    if ctx is not None:
        k_att = jnp.concatenate([kc, ctx['diff_k']], axis=2)
        v_att = jnp.concatenate([vc, ctx['diff_v']], axis=2)
    lam_init = 0.8 - 0.6 * math.exp(-0.3 * layer)
    lam = (jnp.exp(jnp.sum((P['lam_q1'] * P['lam_k1']).astype(jnp.float32)))
           - jnp.exp(jnp.sum((P['lam_q2'] * P['lam_k2']).astype(jnp.float32))) + lam_init)
    o1 = attend(q1, k_att[..., :DC], v_att, DC ** -0.5)
    o2 = attend(q2, k_att[..., DC:], v_att, DC ** -0.5)
    oc = o1 - lam.astype(o1.dtype) * o2
    oc = from_heads(rms_norm(oc, P['g_diff']) * (1.0 - lam_init))

    mix = jnp.concatenate([oa * jax.nn.silu(gg), ob * jax.nn.silu(mg), oc * jax.nn.silu(dg)], axis=-1)
    out = rms_norm(mix @ P['w_out'], P['g_post'])
    x_new = x + gate * out
    if ctx is None:
        return x_new, (ckv_n, kpe, kc, vc, jnp.stack([s_f, s_b], axis=1))
    return x_new, None


def setup_inputs(seed: int = 0) -> dict:
    key = jax.random.key(seed)
    ks = jax.random.split(key, 32)
    nrm = lambda k, shape, s=1.0: jax.random.normal(k, shape, jnp.float32) * s
    gain = lambda k, shape: 1.0 + 0.01 * jax.random.normal(k, shape, jnp.float32)
    return {
        "x_prompt": nrm(ks[0], (BATCH, SEQ, D_MODEL)),
        "x_sample": nrm(ks[1], (DEC_BATCH, DEC_SEQ, D_MODEL)),
        "c": nrm(ks[2], (DEC_BATCH, D_MODEL)),
        "cache_mla_ckv": nrm(ks[3], (DEC_BATCH, DEPTH, PAST_LEN, KV_LORA)),
        "cache_mla_kpe": nrm(ks[4], (DEC_BATCH, DEPTH, PAST_LEN, ROPE_DIM)),
        "cache_diff_k": nrm(ks[5], (DEC_BATCH, DEPTH, H_C, PAST_LEN, 2 * DC)),
        "cache_diff_v": nrm(ks[6], (DEC_BATCH, DEPTH, H_C, PAST_LEN, 2 * DC)),
        "state_gla": nrm(ks[7], (DEC_BATCH, DEPTH, 2, H_A, DK_A, DV_A)),
        "c_ctx": nrm(ks[8], (D_MODEL,)),
        "w_ada": nrm(ks[9], (DEPTH, D_MODEL, 3 * D_MODEL), 0.5 * D_MODEL ** -0.5),
        "b_ada": nrm(ks[10], (DEPTH, 3 * D_MODEL), 0.01),
        "g_pre": gain(ks[11], (DEPTH, D_MODEL)),
        "g_post": gain(ks[12], (DEPTH, D_MODEL)),
        "w_in": nrm(ks[13], (DEPTH, D_MODEL, N_IN), D_MODEL ** -0.5),
        "w_gla_af": nrm(ks[14], (DEPTH, GLA_LR, H_A * DK_A), GLA_LR ** -0.5),
        "b_gla_af": nrm(ks[15], (DEPTH, H_A * DK_A), 0.1),
        "w_gla_ab": nrm(ks[16], (DEPTH, GLA_LR, H_A * DK_A), GLA_LR ** -0.5),
        "b_gla_ab": nrm(ks[17], (DEPTH, H_A * DK_A), 0.1),
        "g_gla": gain(ks[18], (DEPTH, DV_A)),
        "g_mla_q": gain(ks[19], (DEPTH, Q_LORA)),
        "w_mla_uq": nrm(ks[20], (DEPTH, Q_LORA, H_B * (DN_B + DR_B)), Q_LORA ** -0.5),
        "g_mla_kv": gain(ks[21], (DEPTH, KV_LORA)),
        "w_mla_ukv": nrm(ks[22], (DEPTH, KV_LORA, H_B * (DN_B + DV_B)), KV_LORA ** -0.5),
        "lam_q1": nrm(ks[23], (DEPTH, DC), 0.1),
        "lam_k1": nrm(ks[24], (DEPTH, DC), 0.1),
        "lam_q2": nrm(ks[25], (DEPTH, DC), 0.1),
        "lam_k2": nrm(ks[26], (DEPTH, DC), 0.1),
        "g_diff": gain(ks[27], (DEPTH, 2 * DC)),
        "w_out": nrm(ks[28], (DEPTH, D_MIX, D_MODEL), D_MIX ** -0.5),
    }


def reference(x_prompt, x_sample, c, cache_mla_ckv, cache_mla_kpe, cache_diff_k, cache_diff_v,
              state_gla, c_ctx, w_ada, b_ada, g_pre, g_post, w_in, w_gla_af, b_gla_af,
              w_gla_ab, b_gla_ab, g_gla, g_mla_q, w_mla_uq, g_mla_kv, w_mla_ukv,
              lam_q1, lam_k1, lam_q2, lam_k2, g_diff, w_out):
    rope = axial_rope_tables(x_sample.shape[1], ROPE_DIM)
    y_p, y_s = x_prompt, x_sample
    ckv_l, kpe_l, dk_l, dv_l, gla_l = [], [], [], [], []
    for l in range(DEPTH):
        P = dict(g_pre=g_pre[l], g_post=g_post[l], w_in=w_in[l],
                 w_gla_af=w_gla_af[l], b_gla_af=b_gla_af[l], w_gla_ab=w_gla_ab[l], b_gla_ab=b_gla_ab[l],
                 g_gla=g_gla[l], g_mla_q=g_mla_q[l], w_mla_uq=w_mla_uq[l], g_mla_kv=g_mla_kv[l],
                 w_mla_ukv=w_mla_ukv[l], lam_q1=lam_q1[l], lam_k1=lam_k1[l], lam_q2=lam_q2[l],
                 lam_k2=lam_k2[l], g_diff=g_diff[l], w_out=w_out[l])
        mod_ctx = (jax.nn.silu(c_ctx) @ w_ada[l] + b_ada[l])[None, None, :]
        y_p, (ckv_n, kpe, kc, vc, gst) = mixer_sublayer(y_p, mod_ctx, P, l, None, None)
        ckv_l.append(ckv_n); kpe_l.append(kpe); dk_l.append(kc); dv_l.append(vc); gla_l.append(gst)
        mod_lat = (jax.nn.silu(c) @ w_ada[l] + b_ada[l])[:, None, :]
        ctx = dict(mla_ckv=cache_mla_ckv[:, l], mla_kpe=cache_mla_kpe[:, l],
                   diff_k=cache_diff_k[:, l], diff_v=cache_diff_v[:, l], gla=state_gla[:, l])
        y_s, _ = mixer_sublayer(y_s, mod_lat, P, l, ctx, rope)
    new_mla_ckv = jnp.stack(ckv_l, axis=1)
    new_mla_kpe = jnp.stack(kpe_l, axis=1)
    new_diff_k = jnp.stack(dk_l, axis=1)
    new_diff_v = jnp.stack(dv_l, axis=1)
    new_state_gla = jnp.stack(gla_l, axis=1)
    return (y_p, y_s, new_mla_ckv, new_mla_kpe, new_diff_k, new_diff_v, new_state_gla)
```

```python
import math
from contextlib import ExitStack

import numpy as np
import ml_dtypes

import concourse.bass as bass
import concourse.mybir as mybir
from concourse.bass_utils import run_bass_kernel_spmd

F32 = mybir.dt.float32
BF16 = mybir.dt.bfloat16
AF = mybir.ActivationFunctionType
ALU = mybir.AluOpType
AX = mybir.AxisListType

T = 1024
D = 1024
NKEY = 1536
EPS = 1e-6
BIG = 8192.0
ENGS = ("pe", "act", "dve", "pool", "sp")

GROUPS = [
    ("A1", "FM", 256), ("A2", "FM", 32), ("A3", "FM", 256), ("A4", "TM", 128), ("A5", "TM", 256),
    ("B1", "FM", 256), ("B2", "FM", 256), ("B3", "FM", 256), ("B4", "TM", 160),
    ("C1", "FM", 256), ("C2", "FM", 256), ("C3", "FM", 256), ("C4", "FM", 256), ("C5", "FM", 256),
    ("C6", "TM", 256), ("C7", "TM", 256),
]
NCOLX = sum(g[2] for g in GROUPS)
GOFF = {}
_o = 0
for _g in GROUPS:
    GOFF[_g[0]] = (_o, _g[2], _g[1])
    _o += _g[2]


class Op:
    __slots__ = ("eng", "fn", "deps", "signal", "ticket", "is_dma", "dsem", "dval", "dprev")

    def __init__(self, eng, fn, is_dma):
        self.eng = eng
        self.fn = fn
        self.deps = []
        self.signal = False
        self.ticket = None
        self.is_dma = is_dma
        self.dsem = None
        self.dval = None
        self.dprev = None


class Prog:
    def __init__(self, n_dma_sems=48):
        self.ops = {e: [] for e in ENGS}
        self.state = {}
        self.n_dma_sems = n_dma_sems
        self.dma_count = 0
        self.dma_last = [None] * n_dma_sems

    stopped = False

    def op(self, eng, fn, reads=(), writes=(), dma=False):
        if self.stopped:
            return None
        o = Op(eng, fn, dma)
        deps = {}
        st = self.state
        for k in reads:
            s = st.get(k)
            if s is not None and s[0] is not None:
                deps[id(s[0])] = s[0]
        for k in writes:
            s = st.get(k)
            if s is not None:
                if s[0] is not None:
                    deps[id(s[0])] = s[0]
                for r in s[1]:
                    deps[id(r)] = r
        o.deps = list(deps.values())
        for d in o.deps:
            d.signal = True
        for k in reads:
            s = st.get(k)
            if s is None:
                st[k] = [None, [o]]
            else:
                s[1].append(o)
        for k in writes:
            st[k] = [o, []]
        if dma:
            slot = self.dma_count % self.n_dma_sems
            self.dma_count += 1
            prev = self.dma_last[slot]
            o.dsem = slot
            o.dval = (prev.dval if prev is not None else 0) + 16
            o.dprev = prev
            self.dma_last[slot] = o
        self.ops[eng].append(o)
        return o

    def emit(self, nc, stack, final_eng="sp"):
        esem = {e: stack.enter_context(nc.semaphore("s_" + e)) for e in ENGS}
        dsem = [stack.enter_context(nc.semaphore("d_%d" % i)) for i in range(self.n_dma_sems)]
        for e in ENGS:
            t = 0
            for o in self.ops[e]:
                if o.is_dma:
                    continue
                if o.signal:
                    t += 1
                    o.ticket = t
        block = stack.enter_context(nc.Block())
        engmap = {"pe": block.tensor, "act": block.scalar, "dve": block.vector,
                  "pool": block.gpsimd, "sp": block.sync}
        n_dma = self.n_dma_sems

        def run_engine(e):
            def body(eng):
                seen_e = {x: 0 for x in ENGS}
                seen_d = [0] * n_dma
                for o in self.ops[e]:
                    need_e = {}
                    need_d = {}
                    for d in o.deps:
                        if d.is_dma:
                            if d.dval > seen_d[d.dsem]:
                                need_d[d.dsem] = max(need_d.get(d.dsem, 0), d.dval)
                        else:
                            if d.ticket > seen_e[d.eng]:
                                need_e[d.eng] = max(need_e.get(d.eng, 0), d.ticket)
                    if o.is_dma and o.dprev is not None:
                        if o.dprev.dval > seen_d[o.dsem]:
                            need_d[o.dsem] = max(need_d.get(o.dsem, 0), o.dprev.dval)
                    for pe_, t in need_e.items():
                        eng.wait_ge(esem[pe_], t)
                        seen_e[pe_] = t
                    for s, v in need_d.items():
                        eng.wait_ge(dsem[s], v)
                        seen_d[s] = v
                    ins = o.fn(eng)
                    if o.is_dma:
                        ins.then_inc(dsem[o.dsem], 16)
                    elif o.signal:
                        ins.then_inc(esem[e], 1)
                if e == final_eng:
                    for s in range(n_dma):
                        last = self.dma_last[s]
                        if last is not None and last.dval > seen_d[s]:
                            eng.wait_ge(dsem[s], last.dval)
            return body

        for e in ENGS:
            engmap[e](run_engine(e))


GRAN = 512
DEBUG = False
STOP = 0


class _Stop(Exception):
    pass


class View:
    def __init__(self, arena, off, shape, dt, tag="A"):
        self.off = off
        self.shape = list(shape)
        self.dt = dt
        self.esz = 4 if dt == F32 else 2
        n = int(np.prod(shape[1:]))
        self.nbytes = n * self.esz
        assert off % 4 == 0 and self.nbytes % 4 == 0, (off, shape)
        ap = arena[0:shape[0], off // 4:(off + self.nbytes) // 4]
        if dt != F32:
            ap = ap.bitcast(dt)
        if len(shape) == 3:
            ap = ap.rearrange("p (a b) -> p a b", a=shape[1])
        elif len(shape) == 4:
            ap = ap.rearrange("p (a b c) -> p a b c", a=shape[1], b=shape[2])
        self.ap = ap
        self.tag = tag
        self.inner = (n // shape[1]) * self.esz if len(shape) >= 3 else self.nbytes

    def k(self, i=None, j=None, p=(0, 128), e=None):
        if e is not None:
            lo, hi = e[0] * self.esz, e[1] * self.esz
        elif i is None:
            lo, hi = 0, self.nbytes
        else:
            if j is None:
                j = i + 1
            lo, hi = i * self.inner, j * self.inner
        lo += self.off
        hi += self.off
        qs = range(p[0] // 32, (p[1] - 1) // 32 + 1)
        return [(self.tag, g, q) for g in range(lo // GRAN, (hi - 1) // GRAN + 1) for q in qs]


def build_program():
    nc = bass.Bass("TRN2", target_bir_lowering=False)
    P = Prog()

    def din(name, shape, dt=F32):
        return nc.dram_tensor(name, list(shape), dt, kind="ExternalInput").ap()

    def dout(name, shape, dt=F32):
        return nc.dram_tensor(name, list(shape), dt, kind="ExternalOutput").ap()

    x_d = din("x", [T, D])
    cpf_d = din("cpack_f", [128, 4388])
    cpb_d = din("cpack_b", [128, 352], BF16)
    wada_d = din("w_ada", [2, D, 3072])
    win_d = din("w_in_x", [2, D, NCOLX])
    wuq_d = din("w_uq_x", [2, 256, 1024])
    wukvk_d = din("w_ukvk", [2, 128, 768])
    wukvv_d = din("w_ukvv", [2, 128, 512])
    wout_d = din("w_out", [2, D, D])
    wgla_d = din("w_gla_x", [2, 33, 256])
    cckv_d = din("c_ckv", [2, 128, 4, 128])
    ckpe_d = din("c_kpe", [2, 128, 4, 32])
    cdk_d = din("c_dk", [2, 128, 4, 256])
    cdv_d = din("c_dv", [2, 128, 4, 256])
    sgla_d = din("s_gla", [2, 2, 128, 64])
    maskq_d = din("maskq", [5, T], BF16)
    maskk_d = din("maskk", [5, NKEY], BF16)

    y_d = dout("y", [T, D])
    ockv_d = dout("o_ckv", [2, T, 128])
    okpe_d = dout("o_kpe", [2, T, 32])
    odk_d = dout("o_dk", [2, T, 256])
    odv_d = dout("o_dv", [2, T, 256])
    ogla_d = dout("o_gla", [2, 2, 4, 128, 64])
    if DEBUG:
        dbgmix_d = dout("dbg_mix", [2, 128, 8, 1024], BF16)

    ARENA_BYTES = 82 * 1024
    PERS_BYTES = 125 * 1024
    with ExitStack() as st:
        arena = st.enter_context(nc.sbuf_tensor("arena", [128, ARENA_BYTES // 4], F32))
        pers = st.enter_context(nc.sbuf_tensor("pers", [128, PERS_BYTES // 4], F32))
        psum = [st.enter_context(nc.psum_tensor("ps%d" % i, [128, 512], F32)) for i in range(8)]
        psb = [p.bitcast(BF16) for p in psum]

        def PK(*banks):
            return [("ps", b) for b in banks]

        pcur = [0]

        def palloc(shape, dt):
            esz = 4 if dt == F32 else 2
            nb = int(np.prod(shape[1:])) * esz
            nb4 = (nb + 3) // 4 * 4
            if nb4 != nb:
                raise ValueError("palloc size", shape)
            v = View(pers, pcur[0], shape, dt, tag="P")
            pcur[0] += nb
            assert pcur[0] <= PERS_BYTES, pcur[0]
            return v

        X = palloc([128, 8, 1024], F32)
        HT = palloc([128, 8, 1024], BF16)
        MIX = palloc([128, 8, 1024], BF16)
        WST = [palloc([128, 8, 256], F32) for _ in range(2)]
        WBF = [palloc([128, 8, 256], BF16) for _ in range(2)]
        WUQ = palloc([128, 2, 1024], BF16)
        WUKVK = palloc([128, 768], BF16)
        WUKVV = palloc([128, 512], BF16)
        WGLA = palloc([33, 256], BF16)
        CB0 = pcur[0]
        IDENT = palloc([128, 128], BF16)
        ONESBD = palloc([128, 128], BF16)
        IPAD = palloc([128, 96], BF16)
        CB1 = pcur[0]
        ONES = palloc([128, 128], BF16)
        CF0 = pcur[0]
        TRI = palloc([128, 4, 128], F32)
        AMASK = palloc([128, 2, 64], F32)
        ROPEF = palloc([128, 2, 1024], F32)
        ROPET = palloc([128, 8, 2, 32], F32)
        HMASK = palloc([128, 4], F32)
        HMF = palloc([128, 4, 128], F32)
        KEEPC = palloc([128, 2, 16], F32)
        KEEPP = palloc([128, 2, 16], F32)
        SC = palloc([128, 8], F32)
        GPRE = palloc([128, 2, 8], F32)
        BADAC = palloc([128, 2, 24], F32)
        GPOSTC = palloc([128, 2, 8], F32)
        GGLA = palloc([128, 2], F32)
        GDIFF = palloc([128, 2], F32)
        GMQ = palloc([128, 2, 2], F32)
        GMKV = palloc([128, 2, 128], F32)
        LAMV = palloc([128, 2, 128], F32)
        CF1 = pcur[0]
        CONSTS_F = [TRI, AMASK, ROPEF, ROPET, HMASK, HMF, KEEPC, KEEPP, SC, GPRE, BADAC, GPOSTC, GGLA, GDIFF, GMQ, GMKV, LAMV]
        MODC = palloc([128, 2, 16], F32)
        ACOL = palloc([128, 2, 8], F32)
        GGB = [palloc([128, 1024], F32) for _ in range(2)]
        GDC = palloc([128, 2], F32)
        LAMS = palloc([128, 8], F32)
        NEGLAM = palloc([128, 2], F32)
        SMALL = palloc([128, 64], F32)
        ONEF = palloc([128, 2], F32)

        def AVW(off, shape, dt):
            v = View(arena, off, shape, dt, tag="A")
            assert off + v.nbytes <= ARENA_BYTES, (off, shape)
            return v

        def dma(eng, out_ap, in_ap, R=(), W=()):
            P.op(eng, lambda e, o=out_ap, i=in_ap: e.dma_start(out=o, in_=i), reads=R, writes=W, dma=True)

        def OP(eng, fn, R=(), W=()):
            P.op(eng, fn, reads=R, writes=W)

        def act(out, in_, func, R, W, scale=1.0, bias=0.0, accum=None, eng="act"):
            def f(e, out=out, in_=in_, func=func, scale=scale, bias=bias, accum=accum):
                kw = {}
                if accum is not None:
                    kw["accum_out"] = accum
                return e.activation(out=out, in_=in_, func=func, bias=bias, scale=scale, **kw)
            P.op("act", f, reads=R, writes=W)

        def tt(eng, out, in0, in1, op, R, W):
            OP(eng, lambda e, out=out, in0=in0, in1=in1, op=op: e.tensor_tensor(out=out, in0=in0, in1=in1, op=op), R, W)

        def ts(eng, out, in0, s1, s2, op0, op1, R, W):
            if op1 is None:
                OP(eng, lambda e, out=out, in0=in0, s1=s1, op0=op0:
                   e.tensor_scalar(out=out, in0=in0, scalar1=s1, scalar2=None, op0=op0), R, W)
            else:
                OP(eng, lambda e, out=out, in0=in0, s1=s1, s2=s2, op0=op0, op1=op1:
                   e.tensor_scalar(out=out, in0=in0, scalar1=s1, scalar2=s2, op0=op0, op1=op1), R, W)

        def stt(eng, out, in0, scalar, in1, op0, op1, R, W):
            OP(eng, lambda e, out=out, in0=in0, scalar=scalar, in1=in1, op0=op0, op1=op1:
               e.scalar_tensor_tensor(out=out, in0=in0, scalar=scalar, in1=in1, op0=op0, op1=op1), R, W)

        def cp(eng, out, in_, R, W):
            if eng == "act":
                act(out, in_, AF.Copy, R, W)
            else:
                OP(eng, lambda e, out=out, in_=in_: e.tensor_copy(out=out, in_=in_), R, W)

        def memset(eng, ap, val, W):
            OP(eng, lambda e, ap=ap, val=val: e.memset(ap, val), (), W)

        def rstd_inplace(ap, keys, scale, n_unused=None):
            act(ap, ap, AF.Ln, keys, keys, scale=scale, bias=EPSB.ap[0:ap.shape[0], 0:1])
            act(ap, ap, AF.Exp, keys, keys, scale=-0.5)

        EPSB = palloc([128, 2], F32)

        def mark(k):
            if STOP == k:
                P.stopped = True

        memset("pool", ONES.ap, 1.0, ONES.k())
        memset("pool", EPSB.ap, EPS, EPSB.k())
        memset("pool", ONEF.ap, 1.0, ONEF.k())
        kcf = []
        for v in CONSTS_F:
            kcf += v.k()
        assert (CF1 - CF0) // 4 == 4388, (CF1 - CF0) // 4
        dma("sp", pers[:, CF0 // 4:CF1 // 4], cpf_d, W=kcf)
        dma("sp", pers[:, CB0 // 4:CB1 // 4].bitcast(BF16), cpb_d, W=IDENT.k() + ONESBD.k() + IPAD.k())
        for t_ in range(8):
            dma("sp", X.ap[:, t_, :], x_d[t_ * 128:(t_ + 1) * 128, :], W=X.k(t_))

        act(SC.ap, SC.ap, AF.Silu, SC.k(), SC.k())

        for l in range(2):
            lam_init = 0.8 - 0.6 * math.exp(-0.3 * l)
            lv = LAMV.ap[:, l, :].rearrange("p (a b) -> p a b", a=4)
            tmp = SMALL.ap[:, 0:64].rearrange("p (a b) -> p a b", a=2)
            tt("dve", tmp[:, 0, :], lv[:, 0, :], lv[:, 1, :], ALU.mult, LAMV.k(), SMALL.k())
            tt("dve", tmp[:, 1, :], lv[:, 2, :], lv[:, 3, :], ALU.mult, LAMV.k(), SMALL.k())
            OP("dve", lambda e, o=LAMS.ap[:, 2 * l:2 * l + 2], i=tmp: e.reduce_sum(out=o, in_=i, axis=AX.X),
               SMALL.k(), LAMS.k())
            act(LAMS.ap[:, 2 * l:2 * l + 2], LAMS.ap[:, 2 * l:2 * l + 2], AF.Exp, LAMS.k(), LAMS.k())
            stt("dve", NEGLAM.ap[:, l:l + 1], LAMS.ap[:, 2 * l + 1:2 * l + 2], -lam_init,
                LAMS.ap[:, 2 * l:2 * l + 1], ALU.add, ALU.subtract, LAMS.k(), NEGLAM.k())
            ts("dve", GDC.ap[:, l:l + 1], GDIFF.ap[:, l:l + 1], 1.0 - lam_init, None, ALU.mult, None,
               GDIFF.k(), GDC.k())

        def prenorm_stats_xn():
            memset("pool", SMALL.ap, 0.0, SMALL.k())
            for t_ in range(8):
                act(MIX.ap[:, t_, :], X.ap[:, t_, :], AF.Square, X.k(t_), MIX.k(t_) + SMALL.k(),
                    accum=SMALL.ap[:, t_:t_ + 1])
            rstd_inplace(SMALL.ap[:, 0:8], SMALL.k(), 1.0 / D)
            for t_ in range(8):
                if t_ % 2 == 0:
                    ts("dve", MIX.ap[:, t_, :], X.ap[:, t_, :], SMALL.ap[:, t_:t_ + 1], None, ALU.mult, None,
                       X.k(t_) + SMALL.k(), MIX.k(t_))
                else:
                    act(MIX.ap[:, t_, :], X.ap[:, t_, :], AF.Copy, X.k(t_) + SMALL.k(), MIX.k(t_),
                        scale=SMALL.ap[:, t_:t_ + 1])

        XNT = AVW(67584, [128, 8, 1024], BF16)
        prenorm_stats_xn()
        for kc in range(8):
            b = 3 + kc % 2

            def tr0(e, kc=kc, b=b):
                r = None
                for t_ in range(8):
                    r = e.transpose(psb[b][:, t_ * 128:(t_ + 1) * 128], MIX.ap[:, t_, kc * 128:(kc + 1) * 128], IDENT.ap)
                return r
            OP("pe", tr0, MIX.k() + IDENT.k(), PK(b))
            cp("dve" if kc % 2 == 0 else "act", XNT.ap[:, kc, :], psb[b][:, 0:1024], PK(b), XNT.k(kc) + PK(b))

        STG = [AVW(i * 12288, [128, 3072], F32) for i in range(4)]
        WB16 = [AVW(49152 + i * 6144, [128, 3072], BF16) for i in range(2)]
        SCBF = AVW(61440, [128, 8], BF16)
        IDF = AVW(61952, [128, 128], F32)
        ONESF = AVW(62464, [128, 128], F32)
        RHJ = [AVW(62976 + i * 512, [128, 128], F32) for i in range(2)]
        GCOL = AVW(64000, [128, 8], F32)
        cp("dve", SCBF.ap, SC.ap, SC.k(), SCBF.k())
        cp("dve", IDF.ap, IDENT.ap, IDENT.k(), IDF.k())
        memset("pool", ONESF.ap, 1.0, ONESF.k())
        for l in range(2):
            for kc in range(8):
                i_ = l * 8 + kc
                s = STG[i_ % 4]
                wb = WB16[i_ % 2]
                dma("sp", s.ap, wada_d[l, kc * 128:(kc + 1) * 128, :], W=s.k())
                cp("act" if i_ % 2 else "dve", wb.ap, s.ap, s.k(), wb.k())

                def mm_ada(e, wb=wb, kc=kc):
                    r = None
                    for j in range(24):
                        r = e.matmul(psum[0][:, j:j + 1], lhsT=wb.ap[:, j * 128:(j + 1) * 128], rhs=SCBF.ap[:, kc:kc + 1],
                                     start=(kc == 0 and j == 0), stop=(kc == 7), skip_group_check=True)
                    return r
                OP("pe", mm_ada, wb.k() + SCBF.k(), PK(0))
            tt("dve", MODC.ap[:, l, :], psum[0][:, 0:16], BADAC.ap[:, l, 0:16], ALU.add,
               PK(0) + BADAC.k(), MODC.k() + PK(0))
            stt("dve", ACOL.ap[:, l, :], MODC.ap[:, l, 8:16], 1.0, GPRE.ap[:, l, :], ALU.add, ALU.mult,
                MODC.k() + GPRE.k(), ACOL.k())
            tt("dve", GCOL.ap, psum[0][:, 16:24], BADAC.ap[:, l, 16:24], ALU.add, PK(0) + BADAC.k(), GCOL.k() + PK(0))
            tt("dve", GCOL.ap, GCOL.ap, GPOSTC.ap[:, l, :], ALU.mult, GCOL.k() + GPOSTC.k(), GCOL.k())
            for j in range(8):
                rj = RHJ[j % 2]
                ts("dve", rj.ap, IDF.ap, GCOL.ap[:, j:j + 1], None, ALU.mult, None, IDF.k() + GCOL.k(), rj.k())
                bk = 1 + j // 4
                OP("pe", lambda e, rj=rj, bk=bk, j=j: e.matmul(psum[bk][:, (j % 4) * 128:(j % 4 + 1) * 128], lhsT=ONESF.ap,
                                                             rhs=rj.ap, start=True, stop=True),
                   rj.k() + ONESF.k(), PK(bk))
            for nb in range(2):
                sl = slice(nb * 512, (nb + 1) * 512)
                cp("dve", GGB[l].ap[:, sl], psum[1 + nb][:, 0:512], PK(1 + nb), GGB[l].k() + PK(1 + nb))

        mark(1)
        wslot = [0]
        castc = [0]

        def cast_eng():
            castc[0] += 1
            return "dve" if castc[0] % 2 else "act"

        def load_dma(l, gname):
            c0, nc_, kind = GOFF[gname]
            s = wslot[0] % 2
            wslot[0] += 1
            src = win_d[l, :, c0:c0 + nc_].rearrange("(k p) c -> p k c", p=128)
            dma("sp", WST[s].ap[:, :, 0:nc_], src, W=WST[s].k())
            return (s, nc_)

        def load_cast(h, eng=None):
            s, nc_ = h
            cp(eng or cast_eng(), WBF[s].ap[:, :, 0:nc_], WST[s].ap[:, :, 0:nc_], WST[s].k(), WBF[s].k())
            return WBF[s]

        def load_group(l, gname):
            return load_cast(load_dma(l, gname))

        psrot = [0]

        def next_bank(lo=0, hi=8):
            b = lo + psrot[0] % (hi - lo)
            psrot[0] += 1
            return b

        def fm_group(wb, ncols, evac, banks=(0, 8)):
            nch = (ncols + 127) // 128
            for ci in range(nch):
                w = min(128, ncols - ci * 128)
                for tc in range(2):
                    b = next_bank(*banks)

                    def mm(e, wb=wb, ci=ci, w=w, tc=tc, b=b):
                        r = None
                        for kc in range(8):
                            r = e.matmul(psum[b][0:w, 0:512], lhsT=wb.ap[:, kc, ci * 128:ci * 128 + w],
                                         rhs=HT.ap[:, kc, tc * 512:(tc + 1) * 512], start=(kc == 0), stop=(kc == 7))
                        return r
                    OP("pe", mm, wb.k() + HT.k(), PK(b))
                    evac(ci, tc, b, w)

        def tm_group(wb, ncols, evac, banks=(0, 8)):
            for tt_ in range(8):
                b = next_bank(*banks)

                def mm(e, wb=wb, tt_=tt_, b=b, ncols=ncols):
                    r = None
                    for kc in range(8):
                        r = e.matmul(psum[b][:, 0:ncols], lhsT=HT.ap[:, kc, tt_ * 128:(tt_ + 1) * 128],
                                     rhs=wb.ap[:, kc, 0:ncols], start=(kc == 0), stop=(kc == 7))
                    return r
                OP("pe", mm, wb.k() + HT.k(), PK(b))
                evac(tt_, b)

        PTB = [AVW(70 * 1024 + i * 1024, [128, 512], BF16) for i in range(4)]
        RBUF = AVW(74 * 1024, [128, 2, 512], F32)

        deferred = []

        def run_deferred(force=False):
            keep = []
            for ent in deferred:
                ent[0] -= 1
                if force or ent[0] <= 0:
                    ent[1]()
                else:
                    keep.append(ent)
            deferred[:] = keep

        def attention_core(items, AQ, AK, Krows, scale, vl, finish, ptb, LA=4):
            steps = []
            for it in items:
                for kt in range(12):
                    steps.append((it, kt))
            n = len(steps)
            sbank = {}
            for i in range(n + LA):
                if i < n:
                    (blk, tc, ob, tag), kt = steps[i]
                    b = i % 5
                    sbank[i] = b

                    def mmS(e, blk=blk, tc=tc, kt=kt, b=b):
                        return e.matmul(psum[b][:, 0:512], lhsT=AK.ap[0:Krows, blk, kt * 128:(kt + 1) * 128],
                                        rhs=AQ.ap[0:Krows, blk, tc * 512:(tc + 1) * 512], start=True, stop=True)
                    OP("pe", mmS, AK.k(blk) + AQ.k(blk), PK(b))
                j = i - LA
                if j >= 0:
                    (blk, tc, ob, tag), kt = steps[j]
                    b = sbank[j]
                    pt = ptb[j % 4]
                    act(pt.ap, psum[b][:, 0:512], AF.Exp, PK(b), pt.k() + PK(b), scale=scale)
                    lhs = vl(tag, kt)

                    def mmO(e, lhs=lhs[0], pt=pt, ob=ob, kt=kt):
                        return e.matmul(psum[ob][:, 0:512], lhsT=lhs, rhs=pt.ap, start=(kt == 0), stop=(kt == 11))
                    OP("pe", mmO, lhs[1] + pt.k(), PK(ob))
                    if kt == 11:
                        finish(blk, tc, ob, tag)
                    run_deferred()

        def load_small(l, first=82432, second=None):
            offs = [first, second if second is not None else first]
            SWs = [AVW(o, [128, 2, 128], F32) for o in offs]
            SWfs = [AVW(o, [128, 384], F32) for o in offs]
            cnt = [0]

            def nxt():
                i = cnt[0] % 2
                cnt[0] += 1
                return SWs[i], SWfs[i]
            for q in range(8):
                sw, _ = nxt()
                dma("sp", sw.ap, wuq_d[l, :, q * 128:(q + 1) * 128].rearrange("(k p) c -> p k c", p=128), W=sw.k())
                cp("pool", WUQ.ap[:, :, q * 128:(q + 1) * 128], sw.ap, sw.k(), WUQ.k())
            for q in range(2):
                _, sf = nxt()
                dma("sp", sf.ap[:, 0:384], wukvk_d[l, :, q * 384:(q + 1) * 384], W=sf.k())
                cp("pool", WUKVK.ap[:, q * 384:(q + 1) * 384], sf.ap[:, 0:384], sf.k(), WUKVK.k())
            for q in range(2):
                _, sf = nxt()
                dma("sp", sf.ap[:, 0:256], wukvv_d[l, :, q * 256:(q + 1) * 256], W=sf.k())
                cp("pool", WUKVV.ap[:, q * 256:(q + 1) * 256], sf.ap[:, 0:256], sf.k(), WUKVV.k())
            _, sf = nxt()
            dma("sp", sf.ap[0:33, 0:256], wgla_d[l], W=sf.k())
            cp("pool", WGLA.ap, sf.ap[0:33, 0:256], sf.k(), WGLA.k())

        load_small(0, first=64512, second=66048)
        for l in range(2):
            if l == 0:
                for kc in range(8):
                    if kc % 2 == 0:
                        ts("dve", HT.ap[:, kc, :], XNT.ap[:, kc, :], ACOL.ap[:, l, kc:kc + 1], MODC.ap[:, l, kc:kc + 1],
                           ALU.mult, ALU.add, XNT.k(kc) + ACOL.k() + MODC.k(), HT.k(kc))
                    else:
                        act(HT.ap[:, kc, :], XNT.ap[:, kc, :], AF.Identity, XNT.k(kc) + ACOL.k() + MODC.k(), HT.k(kc),
                            scale=ACOL.ap[:, l, kc:kc + 1], bias=MODC.ap[:, l, kc:kc + 1])
            else:
                prenorm_stats_xn()
                for kc in range(8):
                    b = kc % 2

                    def tr(e, kc=kc, b=b):
                        r = None
                        for t_ in range(8):
                            r = e.transpose(psb[b][:, t_ * 128:(t_ + 1) * 128], MIX.ap[:, t_, kc * 128:(kc + 1) * 128],
                                            IDENT.ap)
                        return r
                    OP("pe", tr, MIX.k() + IDENT.k(), PK(b))
                    ts("dve", HT.ap[:, kc, :], psb[b][:, 0:1024], ACOL.ap[:, l, kc:kc + 1], MODC.ap[:, l, kc:kc + 1],
                       ALU.mult, ALU.add, PK(b) + ACOL.k() + MODC.k(), HT.k(kc) + PK(b))

            mark(2 + 10 * l)

            QTS = AVW(0, [128, 1024], F32)
            KT_ = AVW(4096, [128, 1024], F32)
            KTOK = AVW(8192, [128, 8, 128], F32)
            VTOK = AVW(12288, [128, 8, 256], BF16)
            GAT = AVW(16384, [33, 1024], BF16)
            SP = AVW(18432, [128, 8, 256], F32)
            EB = AVW(26624, [128, 1024], F32)
            ENB = AVW(30720, [128, 1024], F32)
            QBD = AVW(34816, [128, 4, 1024], BF16)
            KTL = AVW(43008, [128, 1024], BF16)
            ED = AVW(45056, [128, 8, 128], F32)
            KHAT = AVW(49152, [128, 8, 128], BF16)
            ATS = AVW(51200, [128, 8, 4, 128], BF16)
            XSD = [AVW(59392, [128, 9, 64], F32), AVW(61696, [128, 9, 64], F32)]
            EBFD = [AVW(64000, [128, 8, 64], BF16), AVW(65024, [128, 8, 64], BF16)]
            DDC = AVW(66048, [128, 16], F32)
            OACC = AVW(66112, [128, 2, 1024], F32)
            KHBD = View(pers, MIX.off + 2 * 2048, [128, 8, 4, 128], BF16, tag="P")
            VBD = AVW(74304, [128, 8, 4, 128], BF16)

            if l == 0:
                wA1 = load_group(l, "A1")
                wA2 = load_group(l, "A2")
            else:
                wA1, wA2 = pre_next

            def evA1(ci, tc, b, w):
                sl = slice(tc * 512, (tc + 1) * 512)
                if ci == 0:
                    act(QTS.ap[:, sl], psum[b][:, 0:512], AF.Copy, PK(b), QTS.k(e=(tc * 512, tc * 512 + 512)) + PK(b),
                        scale=32.0 ** -0.5)
                else:
                    cp("dve", KT_.ap[:, sl], psum[b][:, 0:512], PK(b), KT_.k(e=(tc * 512, tc * 512 + 512)) + PK(b))
            fm_group(wA1, 256, evA1)
            wA3 = load_group(l, "A3")
            memset("pool", GAT.ap[32:33, :], 1.0, GAT.k())
            memset("pool", ATS.ap, 0.0, ATS.k())
            memset("pool", VBD.ap, 0.0, VBD.k())

            def evA2(ci, tc, b, w):
                sl = slice(tc * 512, (tc + 1) * 512)
                cp("dve", GAT.ap[0:32, sl], psum[b][0:32, 0:512], PK(b), GAT.k() + PK(b))
            fm_group(wA2, 32, evA2)
            wA4 = load_group(l, "A4")

            def evA3(ci, tc, b, w):
                sl = slice(tc * 512, (tc + 1) * 512)
                act(MIX.ap[:, ci, sl], psum[b][:, 0:512], AF.Silu, PK(b),
                    MIX.k(e=(ci * 1024 + tc * 512, ci * 1024 + tc * 512 + 512)) + PK(b))
            fm_group(wA3, 256, evA3)
            wA5 = load_group(l, "A5")

            def evA4(t_, b):
                cp("dve", KTOK.ap[:, t_, :], psum[b][:, 0:128], PK(b), KTOK.k(t_) + PK(b))
            tm_group(wA4, 128, evA4)
            wB4 = load_group(l, "B4")

            def evA5(t_, b):
                cp("act", VTOK.ap[:, t_, :], psum[b][:, 0:256], PK(b), VTOK.k(t_) + PK(b))
                for par in range(2):
                    r0 = 64 * par
                    cp("pool", VBD.ap[r0:r0 + 64, t_, :, 64 * par:64 * par + 64],
                       VTOK.ap[r0:r0 + 64, t_, :].rearrange("p (h d) -> p h d", h=4), VTOK.k(t_),
                       VBD.k(t_, p=(r0, r0 + 64)))
            tm_group(wA5, 256, evA5)
            wB1 = load_group(l, "B1")

            mark(31 + 100 * l)
            for t_ in range(8):
                b = next_bank()
                OP("pe", lambda e, t_=t_, b=b: e.matmul(psum[b][:, 0:256], lhsT=GAT.ap[0:33, t_ * 128:(t_ + 1) * 128],
                                                        rhs=WGLA.ap[0:33, 0:256], start=True, stop=True),
                   GAT.k() + WGLA.k(), PK(b))
                act(SP.ap[:, t_, :], psum[b][:, 0:256], AF.Exp, PK(b), SP.k(t_) + PK(b), scale=-1.0)
            act(SP.ap, SP.ap, AF.Ln, SP.k(), SP.k(), bias=ONEF.ap[:, 0:1])

            mark(32 + 100 * l)
            for ph in range(2):
                jobs = [(0, ph), (1, 1 - ph)]

                def J(dr, hf):
                    tiles = list(range(4 * hf, 4 * hf + 4))
                    chunks = list(range(8 * hf, 8 * hf + 8))
                    order = chunks if dr == 0 else chunks[::-1]
                    return dict(dr=dr, hf=hf, tiles=tiles, order=order, ts=slice(512 * hf, 512 * hf + 512),
                                er=(512 * hf, 512 * hf + 512), i0=(8 * hf if dr == 0 else 8 * (1 - hf)))
                JJ = [J(*j) for j in jobs]

                for jb in JJ:
                    dr, hf = jb["dr"], jb["hf"]

                    def mmB(e, dr=dr, hf=hf, tiles=jb["tiles"]):
                        r = None
                        for t_ in tiles:
                            r = e.matmul(psum[4 + hf][:, (t_ % 4) * 128:(t_ % 4 + 1) * 128],
                                         lhsT=SP.ap[:, t_, dr * 128:(dr + 1) * 128], rhs=TRI.ap[:, 2 * dr, :],
                                         start=True, stop=True)
                        return r
                    OP("pe", mmB, SP.k() + TRI.k(), PK(4 + hf))

                    def mmD(e, dr=dr, hf=hf, tiles=jb["tiles"]):
                        r = None
                        for t_ in tiles:
                            r = e.matmul(psum[6 + hf][:, (t_ % 4) * 128:(t_ % 4 + 1) * 128],
                                         lhsT=TRI.ap[:, 2 * dr + 1, :], rhs=SP.ap[:, t_, dr * 128:(dr + 1) * 128],
                                         start=True, stop=True)
                        return r
                    OP("pe", mmD, SP.k() + TRI.k(), PK(6 + hf))
                for jb in JJ:
                    hf, sl, er = jb["hf"], jb["ts"], jb["er"]
                    act(EB.ap[:, sl], psum[4 + hf][:, 0:512], AF.Exp, PK(4 + hf), EB.k(e=er) + PK(4 + hf))
                    act(ENB.ap[:, sl], psum[4 + hf][:, 0:512], AF.Exp, PK(4 + hf), ENB.k(e=er) + PK(4 + hf), scale=-1.0)
                    act(ED.ap[:, 4 * hf:4 * hf + 4, :], psum[6 + hf][:, 0:512].rearrange("p (a b) -> p a b", a=4),
                        AF.Exp, PK(6 + hf), ED.k(4 * hf, 4 * hf + 4) + PK(6 + hf))
                for jb in JJ:
                    dr, hf, sl, er = jb["dr"], jb["hf"], jb["ts"], jb["er"]
                    edge = 63 if dr == 0 else 0
                    tt("dve", DDC.ap[:, 8 * hf:8 * hf + 8], EB.ap[:, 512 * hf + edge:512 * hf + 512:64],
                       KEEPC.ap[:, dr, 8 * hf:8 * hf + 8], ALU.mult, EB.k(e=er) + KEEPC.k(), DDC.k())
                    tt("dve", EB.ap[:, sl], EB.ap[:, sl], QTS.ap[:, sl], ALU.mult, EB.k(e=er) + QTS.k(e=er) + DDC.k(), EB.k(e=er))
                    for h_ in range(4):
                        act(QBD.ap[:, h_, sl], EB.ap[:, sl], AF.Copy, EB.k(e=er) + HMASK.k(),
                            QBD.k(e=(h_ * 1024 + er[0], h_ * 1024 + er[1])), scale=HMASK.ap[:, h_:h_ + 1])
                    tt("dve", KTL.ap[:, sl], KT_.ap[:, sl], ENB.ap[:, sl], ALU.mult, KT_.k(e=er) + ENB.k(e=er), KTL.k(e=er))
                for jb in JJ:
                    hf = jb["hf"]
                    t0, t1 = 4 * hf, 4 * hf + 4
                    tt("dve", KHAT.ap[:, t0:t1, :], KTOK.ap[:, t0:t1, :], ED.ap[:, t0:t1, :], ALU.mult,
                       KTOK.k(t0, t1) + ED.k(t0, t1), KHAT.k(t0, t1))
                    for h_ in range(4):
                        tt("dve" if h_ < 2 else "pool", KHBD.ap[:, t0:t1, h_, :], KHAT.ap[:, t0:t1, :],
                           HMF.ap[:, h_, :].unsqueeze(1).broadcast_to([128, 4, 128]), ALU.mult,
                           KHAT.k(t0, t1) + HMF.k(), KHBD.k(t0, t1))
                for gi in range(2):
                    for jb in JJ:
                        dr, hf = jb["dr"], jb["hf"]
                        b = hf if gi == 0 else 4 + hf
                        pr0 = 4 * hf + 2 * gi

                        def mmA(e, pr0=pr0, b=b):
                            r = None
                            for pi in range(2):
                                pr = pr0 + pi
                                for par in range(2):
                                    c = 2 * pr + par
                                    r = e.matmul(psum[b][64 * par:64 * par + 64, pi * 256:(pi + 1) * 256],
                                                 lhsT=KTL.ap[:, c * 64:(c + 1) * 64],
                                                 rhs=QBD.ap[:, :, c * 64:(c + 1) * 64], start=True, stop=True)
                            return r
                        OP("pe", mmA, KTL.k(e=jb["er"]) + QBD.k(), PK(b))
                        for par in range(2):
                            r0 = 64 * par
                            tt("dve", ATS.ap[r0:r0 + 64, pr0:pr0 + 2, :, 64 * par:64 * par + 64],
                               psum[b][r0:r0 + 64, 0:512].rearrange("p (a h i) -> p a h i", a=2, h=4),
                               AMASK.ap[r0:r0 + 64, dr, :].unsqueeze(1).unsqueeze(1).broadcast_to([64, 2, 4, 64]), ALU.mult,
                               PK(b) + AMASK.k(), ATS.k(pr0, pr0 + 2, p=(r0, r0 + 64)) + PK(b))
                for jb in JJ:
                    hf = jb["hf"]

                    def mmU(e, tiles=jb["tiles"], hf=hf):
                        r = None
                        for pr in tiles:
                            for h_ in range(4):
                                r = e.matmul(psum[2 + hf][:, (pr % 4) * 128:(pr % 4 + 1) * 128],
                                             lhsT=KHBD.ap[:, pr, h_, :], rhs=VBD.ap[:, pr, h_, :],
                                             start=(h_ == 0), stop=(h_ == 3))
                        return r
                    OP("pe", mmU, KHBD.k(4 * hf, 4 * hf + 4) + VBD.k(4 * hf, 4 * hf + 4), PK(2 + hf))
                for jb in JJ:
                    dr = jb["dr"]
                    XSj = XSD[dr]
                    if ph == 0:
                        dma("sp", XSj.ap[:, 0, :], sgla_d[l, dr], W=XSj.k(0))
                    else:
                        cp("pool", XSj.ap[:, 0, :], XSj.ap[:, 8, :], XSj.k(8), XSj.k(0))
                for j in range(8):
                    for jb in JJ:
                        dr, hf = jb["dr"], jb["hf"]
                        XSj = XSD[dr]
                        c = jb["order"][j]
                        stt("dve", XSj.ap[:, j + 1, :], XSj.ap[:, j, :], DDC.ap[:, c:c + 1],
                            psum[2 + hf][:, (c % 8) * 64:(c % 8 + 1) * 64], ALU.mult, ALU.add,
                            XSj.k(j) + DDC.k() + PK(2 + hf), XSj.k(j + 1) + PK(2 + hf))
                for jb in JJ:
                    dr, hf, i0 = jb["dr"], jb["hf"], jb["i0"]
                    XSj = XSD[dr]
                    tt("dve", EBFD[dr].ap, XSj.ap[:, 0:8, :],
                       KEEPP.ap[:, dr, i0:i0 + 8].unsqueeze(2).broadcast_to([128, 8, 64]),
                       ALU.mult, XSj.k() + KEEPP.k(), EBFD[dr].k())
                    for s_ in (2 * hf, 2 * hf + 1):
                        slot = 4 * (s_ % 2) + 4 if dr == 0 else 8 - 4 * (s_ % 2)
                        dma("sp", ogla_d[l, dr, s_], XSj.ap[:, slot, :], R=XSj.k(slot))
                for jb in JJ:
                    dr, hf, sl = jb["dr"], jb["hf"], jb["ts"]
                    pos = {c: j for j, c in enumerate(jb["order"])}

                    def mmO(e, pos=pos, dr=dr, hf=hf, tiles=jb["tiles"]):
                        r = None
                        for pr in tiles:
                            for h_ in range(4):
                                bank = 4 + (h_ // 2) * 2 + hf
                                col0 = (pr % 4) * 128
                                o = psum[bank][64 * (h_ % 2):64 * (h_ % 2) + 64, col0:col0 + 128]
                                e.matmul(o, lhsT=VTOK.ap[:, pr, 64 * h_:64 * h_ + 64], rhs=ATS.ap[:, pr, h_, :],
                                         start=True, stop=False)
                                for par in range(2):
                                    c = 2 * pr + par
                                    o2 = psum[bank][64 * (h_ % 2):64 * (h_ % 2) + 64, col0 + 64 * par:col0 + 64 * par + 64]
                                    r = e.matmul(o2, lhsT=EBFD[dr].ap[:, pos[c], :], rhs=QBD.ap[:, h_, c * 64:(c + 1) * 64],
                                                 start=False, stop=(par == 1))
                        return r
                    OP("pe", mmO, VTOK.k(4 * hf, 4 * hf + 4) + ATS.k(4 * hf, 4 * hf + 4) + EBFD[dr].k() + QBD.k(),
                       PK(4 + hf, 6 + hf))
                    for ch in range(2):
                        bank = 4 + ch * 2 + hf
                        ke = OACC.k(e=(ch * 1024 + 512 * hf, ch * 1024 + 512 * hf + 512))
                        if ph == 0:
                            cp("act", OACC.ap[:, ch, sl], psum[bank][:, 0:512], PK(bank), ke + PK(bank))
                        else:
                            tt("dve", OACC.ap[:, ch, sl], OACC.ap[:, ch, sl], psum[bank][:, 0:512], ALU.add,
                               PK(bank) + ke, ke + PK(bank))

            mark(38 + 100 * l)
            SQ = AVW(34816, [128, 2, 1024], BF16)
            RS_ = AVW(26624, [128, 2, 1024], F32)
            act(SQ.ap, OACC.ap, AF.Square, OACC.k(), SQ.k())
            for ch in range(2):
                for tc in range(2):
                    b = next_bank(0, 4)
                    sl = slice(tc * 512, (tc + 1) * 512)
                    OP("pe", lambda e, ch=ch, sl=sl, b=b: e.matmul(psum[b][:, 0:512], lhsT=ONESBD.ap, rhs=SQ.ap[:, ch, sl],
                                                                  start=True, stop=True),
                       SQ.k() + ONESBD.k(), PK(b))
                    ke = RS_.k(e=(ch * 1024 + tc * 512, ch * 1024 + tc * 512 + 512))
                    act(RS_.ap[:, ch, sl], psum[b][:, 0:512], AF.Ln, PK(b), ke + PK(b), scale=1.0 / 64, bias=EPSB.ap[:, 0:1])
                    act(RS_.ap[:, ch, sl], RS_.ap[:, ch, sl], AF.Exp, ke, ke, scale=-0.5)
                    ko = OACC.k(e=(ch * 1024 + tc * 512, ch * 1024 + tc * 512 + 512))
                    tt("dve", OACC.ap[:, ch, sl], OACC.ap[:, ch, sl], RS_.ap[:, ch, sl], ALU.mult, ko + ke, ko)
                    km = MIX.k(e=(ch * 1024 + tc * 512, ch * 1024 + tc * 512 + 512))
                    stt("dve", MIX.ap[:, ch, sl], OACC.ap[:, ch, sl], GGLA.ap[:, l:l + 1], MIX.ap[:, ch, sl],
                        ALU.mult, ALU.mult, ko + km + GGLA.k(), km)

            mark(3 + 10 * l)
            CQG = AVW(0, [128, 2, 1024], BF16)
            SQT = AVW(4096, [128, 2, 1024], BF16)
            RBC = AVW(8192, [128, 1024], F32)
            RC = AVW(12288, [128, 1024], F32)
            RSN = AVW(16384, [128, 1024], F32)
            CKV = AVW(20480, [128, 8, 128], F32)
            CKVB = AVW(24576, [128, 12, 128], BF16)
            CKVT = AVW(27648, [128, 1536], BF16)
            KPE = AVW(30720, [128, 8, 32], F32)
            KPT = AVW(31744, [128, 8, 32], F32)
            KRB = AVW(32768, [128, 12, 32], BF16)
            KPET = AVW(33536, [32, 1536], BF16)
            KPET128 = AVW(33536, [128, 1536], BF16)
            CST1 = AVW(36608, [128, 4, 128], F32)
            CST2 = AVW(38656, [128, 4, 32], F32)
            AQ = AVW(39168, [128, 4, 1024], BF16)
            AK = AVW(47360, [128, 4, 1536], BF16)
            AVM = AVW(59648, [128, 12, 2, 192], BF16)
            QT1s = [AVW(4096, [128, 512], F32), AVW(79872, [128, 512], F32)]
            QT2s = [AVW(6144, [128, 512], F32), AVW(81920, [128, 512], F32)]
            qtc = [0]
            OT = AVW(36608, [128, 512], F32)

            memset("pool", KPET128.ap, 0.0, KPET128.k())

            SSK = SMALL.ap[:, 8:16]

            def evB4(t_, b):
                cp("dve", CKV.ap[:, t_, :], psum[b][:, 0:128], PK(b), CKV.k(t_) + PK(b))
                act(KPT.ap[:, 0:4, :].rearrange("p a b -> p (a b)"), psum[b][:, 0:128], AF.Square, PK(b),
                    KPT.k() + SMALL.k() + PK(b), accum=SMALL.ap[:, 8 + t_:9 + t_])
                cp("dve", KPE.ap[:, t_, :], psum[b][:, 128:160], PK(b), KPE.k(t_) + PK(b))
            tm_group(wB4, 160, evB4)
            wB2 = load_group(l, "B2")
            dma("sp", okpe_d[l].rearrange("(t p) d -> p t d", p=128), KPE.ap, R=KPE.k())

            rstd_inplace(SSK, SMALL.k(), 1.0 / 128)
            tt("dve", CKV.ap, CKV.ap, SSK.unsqueeze(2).broadcast_to([128, 8, 128]), ALU.mult, CKV.k() + SMALL.k(), CKV.k())
            tt("dve", CKV.ap, CKV.ap, GMKV.ap[:, l, :].unsqueeze(1).broadcast_to([128, 8, 128]), ALU.mult,
               CKV.k() + GMKV.k(), CKV.k())
            dma("sp", ockv_d[l].rearrange("(t p) d -> p t d", p=128), CKV.ap, R=CKV.k())
            cp("act", CKVB.ap[:, 0:8, :], CKV.ap, CKV.k(), CKVB.k(0, 8))
            dma("sp", CST1.ap, cckv_d[l], W=CST1.k())
            cp("act", CKVB.ap[:, 8:12, :], CST1.ap, CST1.k(), CKVB.k(8, 12))
            dma("sp", CST2.ap, ckpe_d[l], W=CST2.k())
            cp("act", KRB.ap[:, 8:12, :], CST2.ap, CST2.k(), KRB.k(8, 12))
            kv4 = KPE.ap.rearrange("p t (g s e) -> p t g s e", g=2, s=2)
            tv4 = KPT.ap.rearrange("p t (g s e) -> p t g s e", g=2, s=2)
            sn4 = ROPET.ap[:, :, 1, :].rearrange("p t (g s e) -> p t g s e", g=2, s=2)
            for s_ in range(2):
                for g_ in range(2):
                    tt("dve", tv4[:, :, g_, s_, :], kv4[:, :, g_, 1 - s_, :], sn4[:, :, g_, s_, :], ALU.mult,
                       KPE.k() + ROPET.k(), KPT.k())
            tt("dve", KPE.ap, KPE.ap, ROPET.ap[:, :, 0, :], ALU.mult, KPE.k() + ROPET.k(), KPE.k())
            tt("dve", KRB.ap[:, 0:8, :], KPE.ap, KPT.ap, ALU.add, KPE.k() + KPT.k(), KRB.k(0, 8))

            def evB1(ci, tc, b, w):
                sl = slice(tc * 512, (tc + 1) * 512)
                ke = dict(e=(ci * 1024 + tc * 512, ci * 1024 + tc * 512 + 512))
                ts("dve", CQG.ap[:, ci, sl], psum[b][:, 0:512], GMQ.ap[:, l, ci:ci + 1], None, ALU.mult, None,
                   PK(b) + GMQ.k(), CQG.k(**ke) + PK(b))
                act(SQT.ap[:, ci, sl], psum[b][:, 0:512], AF.Square, PK(b), SQT.k(**ke) + PK(b))
            fm_group(wB1, 256, evB1)
            wB3 = load_group(l, "B3")

            for hb in range(2):
                b = next_bank()
                nt = 8 if hb == 0 else 4

                def trc(e, hb=hb, b=b, nt=nt):
                    r = None
                    for i in range(nt):
                        r = e.transpose(psb[b][:, i * 128:(i + 1) * 128], CKVB.ap[:, hb * 8 + i, :], IDENT.ap)
                    return r
                OP("pe", trc, CKVB.k() + IDENT.k(), PK(b))
                cp("act", CKVT.ap[:, hb * 1024:hb * 1024 + nt * 128], psb[b][:, 0:nt * 128], PK(b),
                   CKVT.k(e=(hb * 1024, hb * 1024 + nt * 128)) + PK(b))
                b2 = next_bank()

                def trk(e, hb=hb, b2=b2, nt=nt):
                    r = None
                    for i in range(nt):
                        r = e.transpose(psb[b2][0:32, i * 128:(i + 1) * 128], KRB.ap[:, hb * 8 + i, :], IDENT.ap)
                    return r
                OP("pe", trk, KRB.k() + IDENT.k(), PK(b2))
                cp("dve", KPET.ap[:, hb * 1024:hb * 1024 + nt * 128], psb[b2][0:32, 0:nt * 128], PK(b2),
                   KPET.k(e=(hb * 1024, hb * 1024 + nt * 128)) + PK(b2))

            for tc in range(2):
                b = next_bank()
                sl = slice(tc * 512, (tc + 1) * 512)

                def mmss(e, sl=sl, b=b):
                    e.matmul(psum[b][:, 0:512], lhsT=ONES.ap, rhs=SQT.ap[:, 0, sl], start=True, stop=False)
                    return e.matmul(psum[b][:, 0:512], lhsT=ONES.ap, rhs=SQT.ap[:, 1, sl], start=False, stop=True)
                OP("pe", mmss, SQT.k() + ONES.k(), PK(b))
                ke = RBC.k(e=(tc * 512, tc * 512 + 512))
                act(RBC.ap[:, sl], psum[b][:, 0:512], AF.Ln, PK(b), ke + PK(b), scale=1.0 / 256, bias=EPSB.ap[:, 0:1])
                act(RBC.ap[:, sl], RBC.ap[:, sl], AF.Exp, ke, ke, scale=-0.5)
            for ci in range(2):
                tt("dve", CQG.ap[:, ci, :], CQG.ap[:, ci, :], RBC.ap, ALU.mult, CQG.k(ci) + RBC.k(), CQG.k(ci))

            def evB23(base):
                def ev(ci, tc, b, w):
                    sl = slice(tc * 512, (tc + 1) * 512)
                    ch = base + ci
                    act(MIX.ap[:, ch, sl], psum[b][:, 0:512], AF.Silu, PK(b),
                        MIX.k(e=(ch * 1024 + tc * 512, ch * 1024 + tc * 512 + 512)) + PK(b))
                return ev
            fm_group(wB2, 256, evB23(2))
            wC1 = load_group(l, "C1")
            fm_group(wB3, 256, evB23(4))
            wC2 = load_group(l, "C2")

            for j in range(4):
                dma("sp", AQ.ap[96:101, j, :], maskq_d, W=AQ.k(j))
                dma("sp", AK.ap[96:101, j, :], maskk_d, W=AK.k(j))
            memset("pool", AVM.ap[:, :, :, 64:128], 1.0, AVM.k())

            for hh in range(2):
                for kt in range(12):
                    b = next_bank(0, 6)
                    OP("pe", lambda e, kt=kt, b=b, hh=hh: e.matmul(psum[b][:, 0:256], lhsT=CKVT.ap[:, kt * 128:(kt + 1) * 128],
                                                                  rhs=WUKVV.ap[:, hh * 256:(hh + 1) * 256], start=True, stop=True),
                       CKVT.k() + WUKVV.k(), PK(b))
                    dst = AVM.ap[:, kt, :, :].rearrange("p a (s d) -> p a s d", s=3)[:, :, 0:3:2, :]
                    cp("act", dst, psum[b][:, 0:256].rearrange("p (a s d) -> p a s d", a=2, s=2),
                       PK(b), AVM.k(kt) + PK(b))
                for j in range(4):
                    h_ = 4 * hh + j
                    for tc in range(2):
                        sl = slice(tc * 512, (tc + 1) * 512)
                        bA = next_bank(0, 6)

                        def mmQ(e, h_=h_, sl=sl, bA=bA):
                            r = None
                            for ci in range(2):
                                r = e.matmul(psum[bA][:, 0:512], lhsT=WUQ.ap[:, ci, 128 * h_:128 * h_ + 128],
                                             rhs=CQG.ap[:, ci, sl], start=(ci == 0), stop=(ci == 1))
                            return r
                        OP("pe", mmQ, WUQ.k() + CQG.k(), PK(bA))
                        QT1, QT2 = QT1s[qtc[0] % 2], QT2s[qtc[0] % 2]
                        qtc[0] += 1
                        er = (j * 1024 + tc * 512, j * 1024 + tc * 512 + 512)
                        cp("act" if tc == 0 else "dve", AQ.ap[0:64, j, sl], psum[bA][0:64, 0:512], PK(bA),
                           AQ.k(e=er, p=(0, 64)) + PK(bA))
                        tt("dve", QT1.ap[64:96, :], psum[bA][64:96, 0:512], ROPEF.ap[64:96, 0, sl], ALU.mult,
                           PK(bA) + ROPEF.k(), QT1.k() + PK(bA))
                        tt("dve", QT2.ap[64:96, :], psum[bA][96:128, 0:512], ROPEF.ap[96:128, 1, sl], ALU.mult,
                           PK(bA) + ROPEF.k(), QT2.k() + PK(bA))
                        tt("pool", AQ.ap[64:96, j, sl], QT1.ap[64:96, :], QT2.ap[64:96, :], ALU.add,
                           QT1.k() + QT2.k(), AQ.k(e=er, p=(64, 96)))
                    for k3 in range(3):
                        sl = slice(k3 * 512, (k3 + 1) * 512)
                        b = next_bank(0, 6)

                        def mmK(e, h_=h_, sl=sl, b=b):
                            e.matmul(psum[b][0:96, 0:512], lhsT=WUKVK.ap[:, 96 * h_:96 * h_ + 96], rhs=CKVT.ap[:, sl],
                                     start=True, stop=False)
                            return e.matmul(psum[b][0:96, 0:512], lhsT=IPAD.ap[:, 0:96], rhs=KPET128.ap[:, sl],
                                            start=False, stop=True)
                        OP("pe", mmK, WUKVK.k() + CKVT.k() + IPAD.k() + KPET128.k(), PK(b))
                        cp("act", AK.ap[0:96, j, sl], psum[b][0:96, 0:512], PK(b),
                           AK.k(e=(j * 1536 + k3 * 512, j * 1536 + k3 * 512 + 512), p=(0, 96)) + PK(b))

                def vl_mla(tag, kt):
                    jj = tag
                    pair, odd = jj // 2, jj % 2
                    ap = AVM.ap[:, kt, pair, 64 * odd:64 * odd + 128]
                    return ap, AVM.k(kt)

                def fin_mla(blk, tc, ob, tag, hh=hh):
                    h_ = 4 * hh + blk
                    odd = h_ % 2
                    dlo, slo = (0, 64) if odd == 0 else (64, 0)
                    ch = 2 + h_ // 2
                    sl = slice(tc * 512, (tc + 1) * 512)
                    OP("dve", lambda e, ob=ob, slo=slo: e.reciprocal(out=RBUF.ap[slo:slo + 64, 0, :],
                                                                   in_=psum[ob][slo:slo + 64, 0:512]),
                       PK(ob), RBUF.k(0) + PK(ob))
                    tt("dve", OT.ap[dlo:dlo + 64, :], psum[ob][dlo:dlo + 64, 0:512], RBUF.ap[slo:slo + 64, 0, :], ALU.mult,
                       PK(ob) + RBUF.k(0), OT.k() + PK(ob))
                    km = MIX.k(e=(ch * 1024 + tc * 512, ch * 1024 + tc * 512 + 512), p=(dlo, dlo + 64))
                    tt("pool", MIX.ap[dlo:dlo + 64, ch, sl], MIX.ap[dlo:dlo + 64, ch, sl], OT.ap[dlo:dlo + 64, :], ALU.mult,
                       km + OT.k(), km)

                items = []
                idx = 0
                for j in range(4):
                    for tc in range(2):
                        items.append((j, tc, 6 + idx % 2, j))
                        idx += 1
                attention_core(items, AQ, AK, 101, 96.0 ** -0.5, vl_mla, fin_mla, PTB)

            mark(4 + 10 * l)
            DQR = AVW(0, [128, 2, 1024], BF16)
            DKR = AVW(4096, [128, 2, 1536], BF16)
            DT1s = [AVW(10240, [128, 512], F32), AVW(14336, [128, 512], F32)]
            DT2s = [AVW(12288, [128, 512], F32), AVW(16384, [128, 512], F32)]
            dtc = [0]
            CDK = AVW(14336, [128, 4, 256], F32)
            CDKB = AVW(18432, [128, 4, 256], BF16)
            CDV = AVW(20480, [128, 4, 256], F32)
            AVD = AVW(24576, [128, 12, 2, 192], BF16)
            OSTG = [AVW(33792 + i * 1024, [128, 256], F32) for i in range(4)]
            AQD = AVW(37888, [128, 2, 1024], BF16)
            AKD = AVW(41984, [128, 2, 1536], BF16)
            OCB = AVW(48128, [128, 1024], F32)
            OCA = AVW(52224, [128, 512], F32)
            OCT = AVW(54272, [128, 512], F32)
            SQD = AVW(56320, [128, 512], BF16)
            RSD = AVW(57344, [128, 512], F32)
            RBUFD = AVW(14336, [128, 2, 512], F32)
            PTD = [AVW(20480 + i * 1024, [128, 512], BF16) for i in range(4)]
            SQD2 = SQD

            def evC12(g):
                def ev(ci, tc, b, w):
                    sl = slice(tc * 512, (tc + 1) * 512)
                    DT1, DT2 = DT1s[dtc[0] % 2], DT2s[dtc[0] % 2]
                    if ci == 0:
                        tt("dve", DT1.ap, psum[b][:, 0:512], ROPEF.ap[:, 0, sl], ALU.mult, PK(b) + ROPEF.k(), DT1.k() + PK(b))
                    else:
                        tt("dve", DT2.ap, psum[b][:, 0:512], ROPEF.ap[:, 1, sl], ALU.mult, PK(b) + ROPEF.k(), DT2.k() + PK(b))
                        tt("pool", DQR.ap[:, g, sl], DT1.ap, DT2.ap, ALU.add, DT1.k() + DT2.k(),
                           DQR.k(e=(g * 1024 + tc * 512, g * 1024 + tc * 512 + 512)))
                        dtc[0] += 1
                return ev

            def evC34(g):
                def ev(ci, tc, b, w):
                    sl = slice(tc * 512, (tc + 1) * 512)
                    DT1, DT2 = DT1s[dtc[0] % 2], DT2s[dtc[0] % 2]
                    if ci == 0:
                        tt("dve", DT1.ap, psum[b][:, 0:512], ROPEF.ap[:, 0, sl], ALU.mult, PK(b) + ROPEF.k(), DT1.k() + PK(b))
                    else:
                        tt("dve", DT2.ap, psum[b][:, 0:512], ROPEF.ap[:, 1, sl], ALU.mult, PK(b) + ROPEF.k(), DT2.k() + PK(b))
                        tt("pool", DKR.ap[:, g, sl], DT1.ap, DT2.ap, ALU.add, DT1.k() + DT2.k(),
                           DKR.k(e=(g * 1536 + tc * 512, g * 1536 + tc * 512 + 512)))
                        dtc[0] += 1
                return ev

            def fm_group_pair(wb, evac):
                for tc in range(2):
                    for ci in range(2):
                        b = next_bank()

                        def mm(e, wb=wb, ci=ci, tc=tc, b=b):
                            r = None
                            for kc in range(8):
                                r = e.matmul(psum[b][:, 0:512], lhsT=wb.ap[:, kc, ci * 128:ci * 128 + 128],
                                             rhs=HT.ap[:, kc, tc * 512:(tc + 1) * 512], start=(kc == 0), stop=(kc == 7))
                            return r
                        OP("pe", mm, wb.k() + HT.k(), PK(b))
                        evac(ci, tc, b, 128)

            WOUT = AVW(59392, [128, 8, 1024], BF16)
            AQS = [AQD, AVW(33792, [128, 2, 1024], BF16)]
            AKS = [AKD, AVW(75776, [128, 2, 1536], BF16)]

            dma("sp", CDK.ap, cdk_d[l], W=CDK.k())
            dma("sp", CDV.ap, cdv_d[l], W=CDV.k())
            cp("dve", CDKB.ap, CDK.ap, CDK.k(), CDKB.k())
            bt = next_bank()

            def trdk(e, bt=bt):
                r = None
                for g in range(2):
                    for kt in range(4):
                        r = e.transpose(psb[bt][:, (g * 4 + kt) * 128:(g * 4 + kt + 1) * 128],
                                        CDKB.ap[:, kt, g * 128:(g + 1) * 128], IDENT.ap)
                return r
            OP("pe", trdk, CDKB.k() + IDENT.k(), PK(bt))
            for g in range(2):
                cp("dve", DKR.ap[:, g, 1024:1536], psb[bt][:, g * 512:(g + 1) * 512], PK(bt),
                   DKR.k(e=(g * 1536 + 1024, g * 1536 + 1536)) + PK(bt))

            def prep_set(si):
                memset("dve" if si == 0 else "pool", AQS[si].ap, 0.0, AQS[si].k())
                memset("pool" if si == 0 else "dve", AKS[si].ap, 0.0, AKS[si].k())
                for c_ in range(2):
                    dma("sp", AQS[si].ap[32:37, c_, :], maskq_d, W=AQS[si].k(c_))
                    dma("sp", AKS[si].ap[32:37, c_, :], maskk_d, W=AKS[si].k(c_))

            def relayout(h_):
                si = h_ % 2
                odd, g = h_ % 2, h_ // 2
                for c_ in range(2):
                    r0 = odd * 64 + c_ * 32
                    dma("sp", AQS[si].ap[0:32, c_, :], DQR.ap[r0:r0 + 32, g, :], R=DQR.k(g), W=AQS[si].k(c_))
                    dma("sp", AKS[si].ap[0:32, c_, :], DKR.ap[r0:r0 + 32, g, :], R=DKR.k(g), W=AKS[si].k(c_))

            prep_set(0)

            fm_group_pair(wC1, evC12(0))
            wC3 = load_group(l, "C3")
            fm_group_pair(wC2, evC12(1))
            wC4 = load_group(l, "C4")
            fm_group_pair(wC3, evC34(0))
            wC7 = load_group(l, "C7")
            fm_group_pair(wC4, evC34(1))
            relayout(0)
            wC5 = load_group(l, "C5")

            ostg_i = [0]
            memset("pool", AVD.ap[:, :, :, 64:128], 1.0, AVD.k())
            dstc = AVD.ap[:, 8:12, :, :].rearrange("p t a (s d) -> p t a s d", s=3)[:, :, :, 0:3:2, :]
            cp("act", dstc, CDV.ap.rearrange("p t (a s d) -> p t a s d", a=2, s=2), CDV.k(), AVD.k(8, 12))

            def evC7(t_, b):
                s = OSTG[ostg_i[0] % 4]
                ostg_i[0] += 1
                cp("dve", s.ap, psum[b][:, 0:256], PK(b), s.k() + PK(b))
                dma("sp", odv_d[l, t_ * 128:(t_ + 1) * 128, :], s.ap, R=s.k())
                dst = AVD.ap[:, t_, :, :].rearrange("p a (s d) -> p a s d", s=3)[:, :, 0:3:2, :]
                cp("act", dst, psum[b][:, 0:256].rearrange("p (a s d) -> p a s d", a=2, s=2), PK(b), AVD.k(t_) + PK(b))
            tm_group(wC7, 256, evC7)
            wC6 = load_group(l, "C6")

            def evC5(ci, tc, b, w):
                sl = slice(tc * 512, (tc + 1) * 512)
                ch = 6 + ci
                act(MIX.ap[:, ch, sl], psum[b][:, 0:512], AF.Silu, PK(b),
                    MIX.k(e=(ch * 1024 + tc * 512, ch * 1024 + tc * 512 + 512)) + PK(b))
            fm_group(wC5, 256, evC5)

            def evC6(t_, b):
                s = OSTG[ostg_i[0] % 4]
                ostg_i[0] += 1
                cp("dve", s.ap, psum[b][:, 0:256], PK(b), s.k() + PK(b))
                dma("sp", odk_d[l, t_ * 128:(t_ + 1) * 128, :], s.ap, R=s.k())
            tm_group(wC6, 256, evC6)

            prep_set(1)
            relayout(1)

            def wout_dma(g4):
                s = wslot[0] % 2
                wslot[0] += 1
                dma("sp", WST[s].ap, wout_d[l, :, g4 * 256:(g4 + 1) * 256].rearrange("(k p) c -> p k c", p=128), W=WST[s].k())
                return (s, g4)

            def wout_cast(h):
                s, g4 = h
                wk = [("WOUTg", g4)] + (WOUT.k() if g4 == 0 else [])
                rk = WST[s].k() + ([] if g4 == 0 else WOUT.k())
                cp("dve" if g4 % 2 == 0 else "pool", WOUT.ap[:, :, g4 * 256:(g4 + 1) * 256], WST[s].ap, rk, wk)

            pend = [wout_dma(0), wout_dma(1)]

            for h_ in range(4):
                odd = h_ % 2
                g = h_ // 2

                def vl_d(tag, kt, h_=h_):
                    pair, od = h_ // 2, h_ % 2
                    return AVD.ap[:, kt, pair, 64 * od:64 * od + 128], AVD.k(kt)

                def fin_d(blk, tc, ob, tag, h_=h_):
                    od = h_ % 2
                    dlo, slo = (0, 64) if od == 0 else (64, 0)
                    pair = h_ // 2
                    sl = slice(tc * 512, (tc + 1) * 512)
                    OP("dve", lambda e, ob=ob, slo=slo, blk=blk: e.reciprocal(out=RBUFD.ap[slo:slo + 64, blk, :],
                                                                            in_=psum[ob][slo:slo + 64, 0:512]),
                       PK(ob), RBUFD.k(blk) + PK(ob))
                    dst = OCA if blk == 0 else OCT
                    tt("dve", dst.ap[dlo:dlo + 64, :], psum[ob][dlo:dlo + 64, 0:512], RBUFD.ap[slo:slo + 64, blk, :], ALU.mult,
                       PK(ob) + RBUFD.k(blk), dst.k() + PK(ob))
                    if blk == 1:
                        ko = OCB.k(e=(tc * 512, tc * 512 + 512), p=(dlo, dlo + 64))
                        stt("dve", OCB.ap[dlo:dlo + 64, sl], OCT.ap[dlo:dlo + 64, :], NEGLAM.ap[dlo:dlo + 64, l:l + 1],
                            OCA.ap[dlo:dlo + 64, :], ALU.mult, ALU.add, OCT.k() + OCA.k() + NEGLAM.k(), ko)
                        if od == 1:
                            def tail(tc=tc, sl=sl, pair=pair):
                                b = 5
                                kob = OCB.k(e=(tc * 512, tc * 512 + 512))
                                act(SQD.ap, OCB.ap[:, sl], AF.Square, kob, SQD.k())
                                OP("pe", lambda e, b=b: e.matmul(psum[b][:, 0:512], lhsT=ONESBD.ap, rhs=SQD.ap, start=True, stop=True),
                                   SQD.k() + ONESBD.k(), PK(b))
                                act(RSD.ap, psum[b][:, 0:512], AF.Ln, PK(b), RSD.k() + PK(b), scale=1.0 / 64, bias=EPSB.ap[:, 0:1])
                                act(RSD.ap, RSD.ap, AF.Exp, RSD.k(), RSD.k(), scale=-0.5)
                                tt("dve", OCB.ap[:, sl], OCB.ap[:, sl], RSD.ap, ALU.mult, kob + RSD.k(), kob)
                                ch = 6 + pair
                                km = MIX.k(e=(ch * 1024 + tc * 512, ch * 1024 + tc * 512 + 512))
                                stt("dve", MIX.ap[:, ch, sl], OCB.ap[:, sl], GDC.ap[:, l:l + 1], MIX.ap[:, ch, sl],
                                    ALU.mult, ALU.mult, kob + km + GDC.k(), km)
                            deferred.append([10, tail])

                items = []
                for tc in range(2):
                    for c_ in range(2):
                        items.append((c_, tc, 6 + c_, c_))
                attention_core(items, AQS[h_ % 2], AKS[h_ % 2], 128, 32.0 ** -0.5, vl_d, fin_d, PTD)
                if h_ + 2 < 4:
                    relayout(h_ + 2)
                if h_ == 0:
                    wout_cast(pend[0]); wout_cast(pend[1])
                    pend = [wout_dma(2), wout_dma(3)]
                    if l == 0:
                        load_small(1, first=82432, second=18432)
                elif h_ == 1:
                    wout_cast(pend[0]); wout_cast(pend[1])
                    if l == 0:
                        pend = [load_dma(1, "A1"), load_dma(1, "A2")]
                elif h_ == 2:
                    if l == 0:
                        pre_next = (load_cast(pend[0], eng="dve"), load_cast(pend[1], eng="pool"))

            run_deferred(force=True)
            mark(5 + 10 * l)
            if DEBUG:
                dma("sp", dbgmix_d[l], MIX.ap, R=MIX.k())
            YT = [AVW(0 + i * 2048, [128, 512], F32) for i in range(4)]
            SQJ = [AVW(8192 + i * 1024, [128, 512], BF16) for i in range(2)]
            for t_ in range(8):
                bs = (next_bank(0, 6), next_bank(0, 6))
                kss = [("opss", l, t_)]
                for nb in range(2):
                    b = bs[nb]

                    def mmo(e, t_=t_, nb=nb, b=b):
                        r = None
                        for kc in range(8):
                            r = e.matmul(psum[b][:, 0:512], lhsT=MIX.ap[:, kc, t_ * 128:(t_ + 1) * 128],
                                         rhs=WOUT.ap[:, kc, nb * 512:(nb + 1) * 512], start=(kc == 0), stop=(kc == 7))
                        return r
                    OP("pe", mmo, MIX.k() + WOUT.k() + [("WOUTg", g_) for g_ in range(4)], PK(b))
                    sq = SQJ[nb]
                    act(sq.ap, psum[b][:, 0:512], AF.Square, PK(b) + SMALL.k(), sq.k() + [("opss", l, t_, nb)] + PK(b),
                        accum=SMALL.ap[:, 16 + 2 * t_ + nb:17 + 2 * t_ + nb])
                s0 = SMALL.ap[:, 16 + 2 * t_:17 + 2 * t_]
                s1 = SMALL.ap[:, 17 + 2 * t_:18 + 2 * t_]
                tt("dve", s0, s0, s1, ALU.add, [("opss", l, t_, 0), ("opss", l, t_, 1)] + SMALL.k(), kss)
                act(s0, s0, AF.Ln, kss + SMALL.k(), kss, scale=1.0 / D, bias=EPSB.ap[:, 0:1])
                act(s0, s0, AF.Exp, kss + SMALL.k(), kss, scale=-0.5)
                for nb in range(2):
                    b = bs[nb]
                    y = YT[(2 * t_ + nb) % 4]
                    sl = slice(nb * 512, (nb + 1) * 512)
                    stt("dve", y.ap, psum[b][:, 0:512], s0, GGB[l].ap[:, sl], ALU.mult, ALU.mult,
                        PK(b) + kss + SMALL.k() + GGB[l].k(), y.k() + PK(b))
                    kx = X.k(e=(t_ * 1024 + nb * 512, t_ * 1024 + nb * 512 + 512))
                    tt("pool" if nb else "dve", X.ap[:, t_, sl], X.ap[:, t_, sl], y.ap, ALU.add, kx + y.k(), kx)
                if l == 1:
                    dma("sp", y_d[t_ * 128:(t_ + 1) * 128, :], X.ap[:, t_, :], R=X.k(t_))
            mark(6 + 10 * l)

        P.emit(nc, st, final_eng="sp")
    return nc


_IN_SIZES = (128, 128, 256, 32, 256, 256, 128, 32, 512, 256, 256, 256, 256)
_IN_NAMES = ("gq", "gk", "gv", "ga", "gg", "cq", "ckv", "kpe", "mg", "dq", "dk", "dv", "dg")


def _rope_perm32():
    p = np.arange(32).reshape(2, 2, 8)[:, ::-1, :].reshape(32)
    return p


def _rope_tables():
    n = T
    row = (np.arange(n) // 64).astype(np.float32)
    col = (np.arange(n) % 64).astype(np.float32)
    half = 16
    inv = (1.0 / (np.float32(10000.0) ** (np.arange(0, half, 2, dtype=np.float32) / np.float32(half)))).astype(np.float32)
    ar = row[:, None] * inv
    ac = col[:, None] * inv
    ang = np.concatenate([ar, ar, ac, ac], axis=-1).astype(np.float32)
    cos, sin = np.cos(ang).astype(np.float32), np.sin(ang).astype(np.float32)
    sign = np.tile(np.concatenate([-np.ones(8), np.ones(8)]), 2).astype(np.float32)
    return cos, sin * sign


_NC_CACHE = {}


def kernel(x_prompt, x_sample, c, cache_mla_ckv, cache_mla_kpe, cache_diff_k, cache_diff_v,
           state_gla, c_ctx, w_ada, b_ada, g_pre, g_post, w_in, w_gla_af, b_gla_af,
           w_gla_ab, b_gla_ab, g_gla, g_mla_q, w_mla_uq, g_mla_kv, w_mla_ukv,
           lam_q1, lam_k1, lam_q2, lam_k2, g_diff, w_out):
    f32 = np.float32
    bf = ml_dtypes.bfloat16
    A = lambda a: np.ascontiguousarray(np.asarray(a, dtype=f32))
    x_prompt, x_sample, c, c_ctx = A(x_prompt), A(x_sample), A(c), A(c_ctx)
    w_in, w_ada, w_out = A(w_in), A(w_ada), A(w_out)
    w_mla_uq, w_mla_ukv = A(w_mla_uq), A(w_mla_ukv)

    offs = np.cumsum((0,) + _IN_SIZES)
    col = {n: (int(offs[i]), int(offs[i + 1])) for i, n in enumerate(_IN_NAMES)}
    perm32 = _rope_perm32()

    def cols(name, a=None, b=None):
        lo, hi = col[name]
        idx = np.arange(lo, hi)
        return idx if a is None else idx[a:b]

    def permuted(idx):
        return idx.reshape(-1, 32)[:, perm32].reshape(-1)

    dq, dk = cols("dq"), cols("dk")
    order = {
        "A1": np.concatenate([cols("gq"), cols("gk")]), "A2": cols("ga"), "A3": cols("gg"),
        "A4": cols("gk"), "A5": cols("gv"),
        "B1": cols("cq"), "B2": cols("mg", 0, 256), "B3": cols("mg", 256, 512),
        "B4": np.concatenate([cols("ckv"), cols("kpe")]),
        "C1": np.concatenate([dq[0:128], permuted(dq[0:128])]),
        "C2": np.concatenate([dq[128:256], permuted(dq[128:256])]),
        "C3": np.concatenate([dk[0:128], permuted(dk[0:128])]),
        "C4": np.concatenate([dk[128:256], permuted(dk[128:256])]),
        "C5": cols("dg"), "C6": cols("dk"), "C7": cols("dv"),
    }
    cidx = np.concatenate([order[g[0]] for g in GROUPS])
    assert cidx.shape[0] == NCOLX
    w_in_x = np.ascontiguousarray(w_in[:, :, cidx])

    uq_idx = np.arange(768).reshape(8, 96)
    uq_cols = np.concatenate([uq_idx, uq_idx[:, 64:96][:, perm32]], axis=1).reshape(-1)
    w_uq_x = np.ascontiguousarray(w_mla_uq[:, :, uq_cols])
    ukv = w_mla_ukv.reshape(2, 128, 8, 128)
    w_ukvk = np.zeros((2, 128, 8, 96), f32)
    w_ukvk[:, :, :, 0:64] = ukv[:, :, :, 0:64]
    w_ukvk = w_ukvk.reshape(2, 128, 768)
    w_ukvv = np.ascontiguousarray(ukv[:, :, :, 64:128]).reshape(2, 128, 512)
    w_gla_x = np.zeros((2, 33, 256), f32)
    w_gla_x[:, 0:16, 0:128] = A(w_gla_af)
    w_gla_x[:, 16:32, 128:256] = A(w_gla_ab)
    w_gla_x[:, 32, 0:128] = A(b_gla_af)
    w_gla_x[:, 32, 128:256] = A(b_gla_ab)
    b_ada = A(b_ada)
    b_ada_c = np.ascontiguousarray(b_ada.reshape(2, 24, 128).transpose(0, 2, 1))
    g_pre_c = np.ascontiguousarray(A(g_pre).reshape(2, 8, 128).transpose(0, 2, 1))
    g_post_c = np.ascontiguousarray(A(g_post).reshape(2, 8, 128).transpose(0, 2, 1))
    g_gla_c = np.ascontiguousarray(np.tile(A(g_gla), (1, 2)).T)
    g_diff_c = np.ascontiguousarray(np.tile(A(g_diff), (1, 2)).T)
    g_mla_q_c = np.ascontiguousarray(A(g_mla_q).reshape(2, 2, 128).transpose(0, 2, 1))
    g_mla_kv_r = A(g_mla_kv).reshape(2, 1, 128)
    lam4 = np.ascontiguousarray(np.stack([A(lam_q1), A(lam_k1), A(lam_q2), A(lam_k2)], axis=1).reshape(2, 1, 128))

    jj = np.arange(128)
    same = (jj[:, None] // 64) == (jj[None, :] // 64)
    le = jj[:, None] <= jj[None, :]
    lt = jj[:, None] < jj[None, :]
    sc16 = f32(-1.0 / 16.0)
    tri = np.zeros((128, 4, 128), f32)
    tri[:, 0, :] = (same & le) * sc16
    tri[:, 1, :] = (same & (~le)) * sc16
    tri[:, 2, :] = (same & (~lt)) * sc16
    tri[:, 3, :] = (same & lt) * sc16
    j64 = (jj % 64)[:, None]
    i64 = np.arange(64)[None, :]
    amask = np.stack([(j64 <= i64), (j64 >= i64)], axis=1).astype(f32)
    ident = np.eye(128, dtype=f32).astype(bf)
    onesbd = np.kron(np.eye(2, dtype=f32), np.ones((64, 64), f32)).astype(bf)
    ipad = np.zeros((32, 96), f32)
    ipad[np.arange(32), 64 + np.arange(32)] = 1.0
    ipad = ipad.astype(bf)
    hmask = (jj[:, None] // 32 == np.arange(4)[None, :]).astype(f32)
    hmf = np.ascontiguousarray(np.broadcast_to((np.arange(4)[:, None] == (jj[None, :] // 32)).astype(f32)[None], (128, 4, 128)))

    cos, sins = _rope_tables()

    def rope_pack(cs, sn):
        rF = np.stack([np.tile(cs.T, (4, 1)), np.tile(sn.T, (4, 1))], axis=1).astype(f32)
        rT = np.stack([cs.reshape(8, 128, 32), sn.reshape(8, 128, 32)], axis=2).transpose(1, 0, 2, 3)
        return np.ascontiguousarray(rF), np.ascontiguousarray(rT.astype(f32))

    ropeF_s, ropeT_s = rope_pack(cos, sins)
    ropeF_p, ropeT_p = rope_pack(np.ones_like(cos), np.zeros_like(sins))

    tq = np.arange(T)
    maskq_p = np.zeros((5, T), f32)
    maskk_p = np.zeros((5, NKEY), f32)
    for b in range(4):
        maskq_p[b] = (tq // 256 == b)
        maskk_p[b, 0:T] = (tq // 256 == b) * BIG
    maskq_p[4] = 1.0
    maskk_p[4] = -BIG
    maskq_s = np.zeros((5, T), f32)
    maskk_s = np.zeros((5, NKEY), f32)

    keepC_s = np.ones((128, 2, 16), f32)
    keepC_p = np.ones((128, 2, 16), f32)
    for cch in range(16):
        if cch % 4 == 0 and cch > 0:
            keepC_p[:, 0, cch] = 0.0
        if (cch + 1) % 4 == 0 and cch < 15:
            keepC_p[:, 1, cch] = 0.0
    def toP(kc):
        kp = kc.copy()
        kp[:, 1, :] = kc[:, 1, ::-1]
        return np.ascontiguousarray(kp)
    keepP_s, keepP_p = toP(keepC_s), toP(keepC_p)

    ipad128 = np.zeros((128, 96), bf)
    ipad128[0:32] = ipad
    cpack_b = np.ascontiguousarray(np.concatenate([ident, onesbd, ipad128], axis=1))
    P128 = lambda a: np.asarray(a, f32).reshape(128, -1)
    bc = lambda a: np.broadcast_to(np.asarray(a, f32).reshape(1, 2, 128), (128, 2, 128))

    def cpack_f(ropeF_, ropeT_, keepC_, keepP_, cvecT_):
        parts = [tri, amask, ropeF_, ropeT_, hmask, hmf, keepC_, keepP_, cvecT_,
                 g_pre_c.transpose(1, 0, 2), b_ada_c.transpose(1, 0, 2), g_post_c.transpose(1, 0, 2),
                 g_gla_c, g_diff_c, g_mla_q_c.transpose(1, 0, 2), bc(g_mla_kv_r), bc(lam4)]
        return np.ascontiguousarray(np.concatenate([P128(p) for p in parts], axis=1))

    shared = dict(w_ada=w_ada, w_in_x=w_in_x, w_uq_x=w_uq_x, w_ukvk=w_ukvk, w_ukvv=w_ukvv, w_out=w_out,
                  w_gla_x=w_gla_x, cpack_b=cpack_b)
    z = lambda *s: np.zeros(s, f32)
    in_maps = []
    for core in range(8):
        d = dict(shared)
        if core < 4:
            b = core
            d.update(x=x_sample[b], cpack_f=cpack_f(ropeF_s, ropeT_s, keepC_s, keepP_s, c[b].reshape(8, 128).T),
                     c_ckv=np.ascontiguousarray(A(cache_mla_ckv[b]).reshape(2, 4, 128, 128).transpose(0, 2, 1, 3)),
                     c_kpe=np.ascontiguousarray(A(cache_mla_kpe[b]).reshape(2, 4, 128, 32).transpose(0, 2, 1, 3)),
                     c_dk=np.ascontiguousarray(A(cache_diff_k[b]).reshape(2, 4, 4, 128, 64).transpose(0, 3, 2, 1, 4).reshape(2, 128, 4, 256)),
                     c_dv=np.ascontiguousarray(A(cache_diff_v[b]).reshape(2, 4, 4, 128, 64).transpose(0, 3, 2, 1, 4).reshape(2, 128, 4, 256)),
                     s_gla=A(state_gla[b]).reshape(2, 2, 128, 64),
                     maskq=maskq_s.astype(bf), maskk=maskk_s.astype(bf))
        else:
            j = core - 4
            d.update(x=np.ascontiguousarray(x_prompt[4 * j:4 * j + 4].reshape(T, D)),
                     cpack_f=cpack_f(ropeF_p, ropeT_p, keepC_p, keepP_p, c_ctx.reshape(8, 128).T),
                     c_ckv=z(2, 128, 4, 128), c_kpe=z(2, 128, 4, 32), c_dk=z(2, 128, 4, 256), c_dv=z(2, 128, 4, 256),
                     s_gla=z(2, 2, 128, 64), maskq=maskq_p.astype(bf), maskk=maskk_p.astype(bf))
        in_maps.append(d)

    if "nc" not in _NC_CACHE:
        _NC_CACHE["nc"] = build_program()
    nc = _NC_CACHE["nc"]
    res = run_bass_kernel_spmd(nc, in_maps, core_ids=list(range(8)))
    R = res.results

    y_sample = np.stack([R[b]["y"] for b in range(4)], axis=0).astype(f32)
    y_prompt = np.concatenate([R[4 + j]["y"].reshape(4, 256, D) for j in range(4)], axis=0).astype(f32)

    def gather(name, inner):
        outs = []
        for j in range(4):
            o = R[4 + j][name]
            outs.append(o.reshape(2, 4, 256, inner).transpose(1, 0, 2, 3))
        return np.concatenate(outs, axis=0)
    new_ckv = np.ascontiguousarray(gather("o_ckv", 128)).astype(f32)
    new_kpe = np.ascontiguousarray(gather("o_kpe", 32)).astype(f32)
    dk_ = gather("o_dk", 256).reshape(16, 2, 256, 4, 64).transpose(0, 1, 3, 2, 4)
    dv_ = gather("o_dv", 256).reshape(16, 2, 256, 4, 64).transpose(0, 1, 3, 2, 4)
    new_dk = np.ascontiguousarray(dk_).astype(f32)
    new_dv = np.ascontiguousarray(dv_).astype(f32)
    gl = []
    for j in range(4):
        o = R[4 + j]["o_gla"]
        gl.append(o.transpose(2, 0, 1, 3, 4).reshape(4, 2, 2, 4, 32, 64))
    new_gla = np.ascontiguousarray(np.concatenate(gl, axis=0)).astype(f32)
    return (y_prompt, y_sample, new_ckv, new_kpe, new_dk, new_dv, new_gla)
```

```python
import math
from contextlib import ExitStack

import numpy as np
import ml_dtypes

import concourse.bass as bass
import concourse.mybir as mybir
from concourse.bass_utils import run_bass_kernel_spmd

F32 = mybir.dt.float32
BF16 = mybir.dt.bfloat16
AF = mybir.ActivationFunctionType
ALU = mybir.AluOpType
AX = mybir.AxisListType

T = 1024
D = 1024
NKEY = 1536
EPS = 1e-6
BIG = 8192.0
ENGS = ("pe", "act", "dve", "pool", "sp")

GROUPS = [
    ("A1", "FM", 256), ("A2", "FM", 32), ("A3", "FM", 256), ("A4", "TM", 128), ("A5", "TM", 256),
    ("B1", "FM", 256), ("B2", "FM", 256), ("B3", "FM", 256), ("B4", "TM", 160),
    ("C1", "FM", 256), ("C2", "FM", 256), ("C3", "FM", 256), ("C4", "FM", 256), ("C5", "FM", 256),
    ("C6", "TM", 256), ("C7", "TM", 256),
]
NCOLX = sum(g[2] for g in GROUPS)
GOFF = {}
_o = 0
for _g in GROUPS:
    GOFF[_g[0]] = (_o, _g[2], _g[1])
    _o += _g[2]


class Op:
    __slots__ = ("eng", "fn", "deps", "signal", "ticket", "is_dma", "dsem", "dval", "dprev")

    def __init__(self, eng, fn, is_dma):
        self.eng = eng
        self.fn = fn
        self.deps = []
        self.signal = False
        self.ticket = None
        self.is_dma = is_dma
        self.dsem = None
        self.dval = None
        self.dprev = None


class Prog:
    def __init__(self, n_dma_sems=48):
        self.ops = {e: [] for e in ENGS}
        self.state = {}
        self.n_dma_sems = n_dma_sems
        self.dma_count = 0
        self.dma_last = [None] * n_dma_sems

    stopped = False

    def op(self, eng, fn, reads=(), writes=(), dma=False):
        if self.stopped:
            return None
        o = Op(eng, fn, dma)
        deps = {}
        st = self.state
        for k in reads:
            s = st.get(k)
            if s is not None and s[0] is not None:
                deps[id(s[0])] = s[0]
        for k in writes:
            s = st.get(k)
            if s is not None:
                if s[0] is not None:
                    deps[id(s[0])] = s[0]
                for r in s[1]:
                    deps[id(r)] = r
        o.deps = list(deps.values())
        for d in o.deps:
            d.signal = True
        for k in reads:
            s = st.get(k)
            if s is None:
                st[k] = [None, [o]]
            else:
                s[1].append(o)
        for k in writes:
            st[k] = [o, []]
        if dma:
            slot = self.dma_count % self.n_dma_sems
            self.dma_count += 1
            prev = self.dma_last[slot]
            o.dsem = slot
            o.dval = (prev.dval if prev is not None else 0) + 16
            o.dprev = prev
            self.dma_last[slot] = o
        self.ops[eng].append(o)
        return o

    def emit(self, nc, stack, final_eng="sp"):
        esem = {e: stack.enter_context(nc.semaphore("s_" + e)) for e in ENGS}
        dsem = [stack.enter_context(nc.semaphore("d_%d" % i)) for i in range(self.n_dma_sems)]
        for e in ENGS:
            t = 0
            for o in self.ops[e]:
                if o.is_dma:
                    continue
                if o.signal:
                    t += 1
                    o.ticket = t
        block = stack.enter_context(nc.Block())
        engmap = {"pe": block.tensor, "act": block.scalar, "dve": block.vector,
                  "pool": block.gpsimd, "sp": block.sync}
        n_dma = self.n_dma_sems

        def run_engine(e):
            def body(eng):
                seen_e = {x: 0 for x in ENGS}
                seen_d = [0] * n_dma
                for o in self.ops[e]:
                    need_e = {}
                    need_d = {}
                    for d in o.deps:
                        if d.is_dma:
                            if d.dval > seen_d[d.dsem]:
                                need_d[d.dsem] = max(need_d.get(d.dsem, 0), d.dval)
                        else:
                            if d.ticket > seen_e[d.eng]:
                                need_e[d.eng] = max(need_e.get(d.eng, 0), d.ticket)
                    if o.is_dma and o.dprev is not None:
                        if o.dprev.dval > seen_d[o.dsem]:
                            need_d[o.dsem] = max(need_d.get(o.dsem, 0), o.dprev.dval)
                    for pe_, t in need_e.items():
                        eng.wait_ge(esem[pe_], t)
                        seen_e[pe_] = t
                    for s, v in need_d.items():
                        eng.wait_ge(dsem[s], v)
                        seen_d[s] = v
                    ins = o.fn(eng)
                    if o.is_dma:
                        ins.then_inc(dsem[o.dsem], 16)
                    elif o.signal:
                        ins.then_inc(esem[e], 1)
                if e == final_eng:
                    for s in range(n_dma):
                        last = self.dma_last[s]
                        if last is not None and last.dval > seen_d[s]:
                            eng.wait_ge(dsem[s], last.dval)
            return body

        for e in ENGS:
            engmap[e](run_engine(e))


GRAN = 512
DEBUG = False
STOP = 0


class _Stop(Exception):
    pass


class View:
    def __init__(self, arena, off, shape, dt, tag="A"):
        self.off = off
        self.shape = list(shape)
        self.dt = dt
        self.esz = 4 if dt == F32 else 2
        n = int(np.prod(shape[1:]))
        self.nbytes = n * self.esz
        assert off % 4 == 0 and self.nbytes % 4 == 0, (off, shape)
        ap = arena[0:shape[0], off // 4:(off + self.nbytes) // 4]
        if dt != F32:
            ap = ap.bitcast(dt)
        if len(shape) == 3:
            ap = ap.rearrange("p (a b) -> p a b", a=shape[1])
        elif len(shape) == 4:
            ap = ap.rearrange("p (a b c) -> p a b c", a=shape[1], b=shape[2])
        self.ap = ap
        self.tag = tag
        self.inner = (n // shape[1]) * self.esz if len(shape) >= 3 else self.nbytes

    def k(self, i=None, j=None, p=(0, 128), e=None):
        if e is not None:
            lo, hi = e[0] * self.esz, e[1] * self.esz
        elif i is None:
            lo, hi = 0, self.nbytes
        else:
            if j is None:
                j = i + 1
            lo, hi = i * self.inner, j * self.inner
        lo += self.off
        hi += self.off
        qs = range(p[0] // 32, (p[1] - 1) // 32 + 1)
        return [(self.tag, g, q) for g in range(lo // GRAN, (hi - 1) // GRAN + 1) for q in qs]


def build_program():
    nc = bass.Bass("TRN2", target_bir_lowering=False)
    P = Prog()

    def din(name, shape, dt=F32):
        return nc.dram_tensor(name, list(shape), dt, kind="ExternalInput").ap()

    def dout(name, shape, dt=F32):
        return nc.dram_tensor(name, list(shape), dt, kind="ExternalOutput").ap()

    x_d = din("x", [T, D])
    cpf_d = din("cpack_f", [128, 4388])
    cpb_d = din("cpack_b", [128, 352], BF16)
    wada_d = din("w_ada", [2, D, 3072])
    win_d = din("w_in_x", [2, D, NCOLX])
    wuq_d = din("w_uq_x", [2, 256, 1024])
    wukvk_d = din("w_ukvk", [2, 128, 768])
    wukvv_d = din("w_ukvv", [2, 128, 512])
    wout_d = din("w_out", [2, D, D])
    wgla_d = din("w_gla_x", [2, 33, 256])
    cckv_d = din("c_ckv", [2, 128, 4, 128])
    ckpe_d = din("c_kpe", [2, 128, 4, 32])
    cdk_d = din("c_dk", [2, 128, 4, 256])
    cdv_d = din("c_dv", [2, 128, 4, 256])
    sgla_d = din("s_gla", [2, 2, 128, 64])
    maskq_d = din("maskq", [5, T], BF16)
    maskk_d = din("maskk", [5, NKEY], BF16)

    y_d = dout("y", [T, D])
    ockv_d = dout("o_ckv", [2, T, 128])
    okpe_d = dout("o_kpe", [2, T, 32])
    odk_d = dout("o_dk", [2, T, 256])
    odv_d = dout("o_dv", [2, T, 256])
    ogla_d = dout("o_gla", [2, 2, 4, 128, 64])
    if DEBUG:
        dbgmix_d = dout("dbg_mix", [2, 128, 8, 1024], BF16)

    ARENA_BYTES = 82 * 1024
    PERS_BYTES = 125 * 1024
    with ExitStack() as st:
        arena = st.enter_context(nc.sbuf_tensor("arena", [128, ARENA_BYTES // 4], F32))
        pers = st.enter_context(nc.sbuf_tensor("pers", [128, PERS_BYTES // 4], F32))
        psum = [st.enter_context(nc.psum_tensor("ps%d" % i, [128, 512], F32)) for i in range(8)]
        psb = [p.bitcast(BF16) for p in psum]

        def PK(*banks):
            return [("ps", b) for b in banks]

        pcur = [0]

        def palloc(shape, dt):
            esz = 4 if dt == F32 else 2
            nb = int(np.prod(shape[1:])) * esz
            nb4 = (nb + 3) // 4 * 4
            if nb4 != nb:
                raise ValueError("palloc size", shape)
            v = View(pers, pcur[0], shape, dt, tag="P")
            pcur[0] += nb
            assert pcur[0] <= PERS_BYTES, pcur[0]
            return v

        X = palloc([128, 8, 1024], F32)
        HT = palloc([128, 8, 1024], BF16)
        MIX = palloc([128, 8, 1024], BF16)
        WST = [palloc([128, 8, 256], F32) for _ in range(2)]
        WBF = [palloc([128, 8, 256], BF16) for _ in range(2)]
        WUQ = palloc([128, 2, 1024], BF16)
        WUKVK = palloc([128, 768], BF16)
        WUKVV = palloc([128, 512], BF16)
        WGLA = palloc([33, 256], BF16)
        CB0 = pcur[0]
        IDENT = palloc([128, 128], BF16)
        ONESBD = palloc([128, 128], BF16)
        IPAD = palloc([128, 96], BF16)
        CB1 = pcur[0]
        ONES = palloc([128, 128], BF16)
        CF0 = pcur[0]
        TRI = palloc([128, 4, 128], F32)
        AMASK = palloc([128, 2, 64], F32)
        ROPEF = palloc([128, 2, 1024], F32)
        ROPET = palloc([128, 8, 2, 32], F32)
        HMASK = palloc([128, 4], F32)
        HMF = palloc([128, 4, 128], F32)
        KEEPC = palloc([128, 2, 16], F32)
        KEEPP = palloc([128, 2, 16], F32)
        SC = palloc([128, 8], F32)
        GPRE = palloc([128, 2, 8], F32)
        BADAC = palloc([128, 2, 24], F32)
        GPOSTC = palloc([128, 2, 8], F32)
        GGLA = palloc([128, 2], F32)
        GDIFF = palloc([128, 2], F32)
        GMQ = palloc([128, 2, 2], F32)
        GMKV = palloc([128, 2, 128], F32)
        LAMV = palloc([128, 2, 128], F32)
        CF1 = pcur[0]
        CONSTS_F = [TRI, AMASK, ROPEF, ROPET, HMASK, HMF, KEEPC, KEEPP, SC, GPRE, BADAC, GPOSTC, GGLA, GDIFF, GMQ, GMKV, LAMV]
        MODC = palloc([128, 2, 16], F32)
        ACOL = palloc([128, 2, 8], F32)
        GGB = [palloc([128, 1024], F32) for _ in range(2)]
        GDC = palloc([128, 2], F32)
        LAMS = palloc([128, 8], F32)
        NEGLAM = palloc([128, 2], F32)
        SMALL = palloc([128, 64], F32)
        ONEF = palloc([128, 2], F32)

        def AVW(off, shape, dt):
            v = View(arena, off, shape, dt, tag="A")
            assert off + v.nbytes <= ARENA_BYTES, (off, shape)
            return v

        def dma(eng, out_ap, in_ap, R=(), W=()):
            P.op(eng, lambda e, o=out_ap, i=in_ap: e.dma_start(out=o, in_=i), reads=R, writes=W, dma=True)

        def OP(eng, fn, R=(), W=()):
            P.op(eng, fn, reads=R, writes=W)

        def act(out, in_, func, R, W, scale=1.0, bias=0.0, accum=None, eng="act"):
            def f(e, out=out, in_=in_, func=func, scale=scale, bias=bias, accum=accum):
                kw = {}
                if accum is not None:
                    kw["accum_out"] = accum
                return e.activation(out=out, in_=in_, func=func, bias=bias, scale=scale, **kw)
            P.op("act", f, reads=R, writes=W)

        def tt(eng, out, in0, in1, op, R, W):
            OP(eng, lambda e, out=out, in0=in0, in1=in1, op=op: e.tensor_tensor(out=out, in0=in0, in1=in1, op=op), R, W)

        def ts(eng, out, in0, s1, s2, op0, op1, R, W):
            if op1 is None:
                OP(eng, lambda e, out=out, in0=in0, s1=s1, op0=op0:
                   e.tensor_scalar(out=out, in0=in0, scalar1=s1, scalar2=None, op0=op0), R, W)
            else:
                OP(eng, lambda e, out=out, in0=in0, s1=s1, s2=s2, op0=op0, op1=op1:
                   e.tensor_scalar(out=out, in0=in0, scalar1=s1, scalar2=s2, op0=op0, op1=op1), R, W)

        def stt(eng, out, in0, scalar, in1, op0, op1, R, W):
            OP(eng, lambda e, out=out, in0=in0, scalar=scalar, in1=in1, op0=op0, op1=op1:
               e.scalar_tensor_tensor(out=out, in0=in0, scalar=scalar, in1=in1, op0=op0, op1=op1), R, W)

        def cp(eng, out, in_, R, W):
            if eng == "act":
                act(out, in_, AF.Copy, R, W)
            else:
                OP(eng, lambda e, out=out, in_=in_: e.tensor_copy(out=out, in_=in_), R, W)

        def memset(eng, ap, val, W):
            OP(eng, lambda e, ap=ap, val=val: e.memset(ap, val), (), W)

        def rstd_inplace(ap, keys, scale, n_unused=None):
            act(ap, ap, AF.Ln, keys, keys, scale=scale, bias=EPSB.ap[0:ap.shape[0], 0:1])
            act(ap, ap, AF.Exp, keys, keys, scale=-0.5)

        EPSB = palloc([128, 2], F32)

        def mark(k):
            if STOP == k:
                P.stopped = True

        memset("pool", ONES.ap, 1.0, ONES.k())
        memset("pool", EPSB.ap, EPS, EPSB.k())
        memset("pool", ONEF.ap, 1.0, ONEF.k())
        kcf = []
        for v in CONSTS_F:
            kcf += v.k()
        assert (CF1 - CF0) // 4 == 4388, (CF1 - CF0) // 4
        dma("sp", pers[:, CF0 // 4:CF1 // 4], cpf_d, W=kcf)
        dma("sp", pers[:, CB0 // 4:CB1 // 4].bitcast(BF16), cpb_d, W=IDENT.k() + ONESBD.k() + IPAD.k())
        for t_ in range(8):
            dma("sp", X.ap[:, t_, :], x_d[t_ * 128:(t_ + 1) * 128, :], W=X.k(t_))

        act(SC.ap, SC.ap, AF.Silu, SC.k(), SC.k())

        for l in range(2):
            lam_init = 0.8 - 0.6 * math.exp(-0.3 * l)
            lv = LAMV.ap[:, l, :].rearrange("p (a b) -> p a b", a=4)
            tmp = SMALL.ap[:, 0:64].rearrange("p (a b) -> p a b", a=2)
            tt("dve", tmp[:, 0, :], lv[:, 0, :], lv[:, 1, :], ALU.mult, LAMV.k(), SMALL.k())
            tt("dve", tmp[:, 1, :], lv[:, 2, :], lv[:, 3, :], ALU.mult, LAMV.k(), SMALL.k())
            OP("dve", lambda e, o=LAMS.ap[:, 2 * l:2 * l + 2], i=tmp: e.reduce_sum(out=o, in_=i, axis=AX.X),
               SMALL.k(), LAMS.k())
            act(LAMS.ap[:, 2 * l:2 * l + 2], LAMS.ap[:, 2 * l:2 * l + 2], AF.Exp, LAMS.k(), LAMS.k())
            stt("dve", NEGLAM.ap[:, l:l + 1], LAMS.ap[:, 2 * l + 1:2 * l + 2], -lam_init,
                LAMS.ap[:, 2 * l:2 * l + 1], ALU.add, ALU.subtract, LAMS.k(), NEGLAM.k())
            ts("dve", GDC.ap[:, l:l + 1], GDIFF.ap[:, l:l + 1], 1.0 - lam_init, None, ALU.mult, None,
               GDIFF.k(), GDC.k())

        def prenorm_stats_xn():
            memset("pool", SMALL.ap, 0.0, SMALL.k())
            for t_ in range(8):
                act(MIX.ap[:, t_, :], X.ap[:, t_, :], AF.Square, X.k(t_), MIX.k(t_) + SMALL.k(),
                    accum=SMALL.ap[:, t_:t_ + 1])
            rstd_inplace(SMALL.ap[:, 0:8], SMALL.k(), 1.0 / D)
            for t_ in range(8):
                if t_ % 2 == 0:
                    ts("dve", MIX.ap[:, t_, :], X.ap[:, t_, :], SMALL.ap[:, t_:t_ + 1], None, ALU.mult, None,
                       X.k(t_) + SMALL.k(), MIX.k(t_))
                else:
                    act(MIX.ap[:, t_, :], X.ap[:, t_, :], AF.Copy, X.k(t_) + SMALL.k(), MIX.k(t_),
                        scale=SMALL.ap[:, t_:t_ + 1])

        XNT = AVW(67584, [128, 8, 1024], BF16)
        prenorm_stats_xn()
        for kc in range(8):
            b = 3 + kc % 2

            def tr0(e, kc=kc, b=b):
                r = None
                for t_ in range(8):
                    r = e.transpose(psb[b][:, t_ * 128:(t_ + 1) * 128], MIX.ap[:, t_, kc * 128:(kc + 1) * 128], IDENT.ap)
                return r
            OP("pe", tr0, MIX.k() + IDENT.k(), PK(b))
            cp("dve" if kc % 2 == 0 else "act", XNT.ap[:, kc, :], psb[b][:, 0:1024], PK(b), XNT.k(kc) + PK(b))

        STG = [AVW(i * 12288, [128, 3072], F32) for i in range(4)]
        WB16 = [AVW(49152 + i * 6144, [128, 3072], BF16) for i in range(2)]
        SCBF = AVW(61440, [128, 8], BF16)
        IDF = AVW(61952, [128, 128], F32)
        ONESF = AVW(62464, [128, 128], F32)
        RHJ = [AVW(62976 + i * 512, [128, 128], F32) for i in range(2)]
        GCOL = AVW(64000, [128, 8], F32)
        cp("dve", SCBF.ap, SC.ap, SC.k(), SCBF.k())
        cp("dve", IDF.ap, IDENT.ap, IDENT.k(), IDF.k())
        memset("pool", ONESF.ap, 1.0, ONESF.k())
        for l in range(2):
            for kc in range(8):
                i_ = l * 8 + kc
                s = STG[i_ % 4]
                wb = WB16[i_ % 2]
                dma("sp", s.ap, wada_d[l, kc * 128:(kc + 1) * 128, :], W=s.k())
                cp("act" if i_ % 2 else "dve", wb.ap, s.ap, s.k(), wb.k())

                def mm_ada(e, wb=wb, kc=kc):
                    r = None
                    for j in range(24):
                        r = e.matmul(psum[0][:, j:j + 1], lhsT=wb.ap[:, j * 128:(j + 1) * 128], rhs=SCBF.ap[:, kc:kc + 1],
                                     start=(kc == 0 and j == 0), stop=(kc == 7), skip_group_check=True)
                    return r
                OP("pe", mm_ada, wb.k() + SCBF.k(), PK(0))
            tt("dve", MODC.ap[:, l, :], psum[0][:, 0:16], BADAC.ap[:, l, 0:16], ALU.add,
               PK(0) + BADAC.k(), MODC.k() + PK(0))
            stt("dve", ACOL.ap[:, l, :], MODC.ap[:, l, 8:16], 1.0, GPRE.ap[:, l, :], ALU.add, ALU.mult,
                MODC.k() + GPRE.k(), ACOL.k())
            tt("dve", GCOL.ap, psum[0][:, 16:24], BADAC.ap[:, l, 16:24], ALU.add, PK(0) + BADAC.k(), GCOL.k() + PK(0))
            tt("dve", GCOL.ap, GCOL.ap, GPOSTC.ap[:, l, :], ALU.mult, GCOL.k() + GPOSTC.k(), GCOL.k())
            for j in range(8):
                rj = RHJ[j % 2]
                ts("dve", rj.ap, IDF.ap, GCOL.ap[:, j:j + 1], None, ALU.mult, None, IDF.k() + GCOL.k(), rj.k())
                bk = 1 + j // 4
                OP("pe", lambda e, rj=rj, bk=bk, j=j: e.matmul(psum[bk][:, (j % 4) * 128:(j % 4 + 1) * 128], lhsT=ONESF.ap,
                                                             rhs=rj.ap, start=True, stop=True),
                   rj.k() + ONESF.k(), PK(bk))
            for nb in range(2):
                sl = slice(nb * 512, (nb + 1) * 512)
                cp("dve", GGB[l].ap[:, sl], psum[1 + nb][:, 0:512], PK(1 + nb), GGB[l].k() + PK(1 + nb))

        mark(1)
        wslot = [0]
        castc = [0]

        def cast_eng():
            castc[0] += 1
            return "dve" if castc[0] % 2 else "act"

        def load_dma(l, gname):
            c0, nc_, kind = GOFF[gname]
            s = wslot[0] % 2
            wslot[0] += 1
            src = win_d[l, :, c0:c0 + nc_].rearrange("(k p) c -> p k c", p=128)
            dma("sp", WST[s].ap[:, :, 0:nc_], src, W=WST[s].k())
            return (s, nc_)

        def load_cast(h, eng=None):
            s, nc_ = h
            cp(eng or cast_eng(), WBF[s].ap[:, :, 0:nc_], WST[s].ap[:, :, 0:nc_], WST[s].k(), WBF[s].k())
            return WBF[s]

        def load_group(l, gname):
            return load_cast(load_dma(l, gname))

        psrot = [0]

        def next_bank(lo=0, hi=8):
            b = lo + psrot[0] % (hi - lo)
            psrot[0] += 1
            return b

        def fm_group(wb, ncols, evac, banks=(0, 8)):
            nch = (ncols + 127) // 128
            for ci in range(nch):
                w = min(128, ncols - ci * 128)
                for tc in range(2):
                    b = next_bank(*banks)

                    def mm(e, wb=wb, ci=ci, w=w, tc=tc, b=b):
                        r = None
                        for kc in range(8):
                            r = e.matmul(psum[b][0:w, 0:512], lhsT=wb.ap[:, kc, ci * 128:ci * 128 + w],
                                         rhs=HT.ap[:, kc, tc * 512:(tc + 1) * 512], start=(kc == 0), stop=(kc == 7))
                        return r
                    OP("pe", mm, wb.k() + HT.k(), PK(b))
                    evac(ci, tc, b, w)

        def tm_group(wb, ncols, evac, banks=(0, 8)):
            for tt_ in range(8):
                b = next_bank(*banks)

                def mm(e, wb=wb, tt_=tt_, b=b, ncols=ncols):
                    r = None
                    for kc in range(8):
                        r = e.matmul(psum[b][:, 0:ncols], lhsT=HT.ap[:, kc, tt_ * 128:(tt_ + 1) * 128],
                                     rhs=wb.ap[:, kc, 0:ncols], start=(kc == 0), stop=(kc == 7))
                    return r
                OP("pe", mm, wb.k() + HT.k(), PK(b))
                evac(tt_, b)

        PTB = [AVW(70 * 1024 + i * 1024, [128, 512], BF16) for i in range(4)]
        RBUF = AVW(74 * 1024, [128, 2, 512], F32)

        deferred = []

        def run_deferred(force=False):
            keep = []
            for ent in deferred:
                ent[0] -= 1
                if force or ent[0] <= 0:
                    ent[1]()
                else:
                    keep.append(ent)
            deferred[:] = keep

        def attention_core(items, AQ, AK, Krows, scale, vl, finish, ptb, LA=4, nsb=5):
            steps = []
            for it in items:
                for kt in range(12):
                    steps.append((it, kt))
            n = len(steps)
            sbank = {}
            for i in range(n + LA):
                if i < n:
                    (blk, tc, ob, tag), kt = steps[i]
                    b = i % nsb
                    sbank[i] = b

                    def mmS(e, blk=blk, tc=tc, kt=kt, b=b):
                        return e.matmul(psum[b][:, 0:512], lhsT=AK.ap[0:Krows, blk, kt * 128:(kt + 1) * 128],
                                        rhs=AQ.ap[0:Krows, blk, tc * 512:(tc + 1) * 512], start=True, stop=True)
                    OP("pe", mmS, AK.k(blk) + AQ.k(blk), PK(b))
                j = i - LA
                if j >= 0:
                    (blk, tc, ob, tag), kt = steps[j]
                    b = sbank[j]
                    pt = ptb[j % 4]
                    act(pt.ap, psum[b][:, 0:512], AF.Exp, PK(b), pt.k() + PK(b), scale=scale)
                    lhs = vl(tag, kt)

                    def mmO(e, lhs=lhs[0], pt=pt, ob=ob, kt=kt):
                        return e.matmul(psum[ob][:, 0:512], lhsT=lhs, rhs=pt.ap, start=(kt == 0), stop=(kt == 11))
                    OP("pe", mmO, lhs[1] + pt.k(), PK(ob))
                    if kt == 11:
                        finish(blk, tc, ob, tag)
                    run_deferred()

        def load_small(l, first=82432, second=None):
            offs = [first, second if second is not None else first]
            SWs = [AVW(o, [128, 2, 128], F32) for o in offs]
            SWfs = [AVW(o, [128, 384], F32) for o in offs]
            cnt = [0]

            def nxt():
                i = cnt[0] % 2
                cnt[0] += 1
                return SWs[i], SWfs[i]
            for q in range(8):
                sw, _ = nxt()
                dma("sp", sw.ap, wuq_d[l, :, q * 128:(q + 1) * 128].rearrange("(k p) c -> p k c", p=128), W=sw.k())
                cp("pool", WUQ.ap[:, :, q * 128:(q + 1) * 128], sw.ap, sw.k(), WUQ.k())
            for q in range(2):
                _, sf = nxt()
                dma("sp", sf.ap[:, 0:384], wukvk_d[l, :, q * 384:(q + 1) * 384], W=sf.k())
                cp("pool", WUKVK.ap[:, q * 384:(q + 1) * 384], sf.ap[:, 0:384], sf.k(), WUKVK.k())
            for q in range(2):
                _, sf = nxt()
                dma("sp", sf.ap[:, 0:256], wukvv_d[l, :, q * 256:(q + 1) * 256], W=sf.k())
                cp("pool", WUKVV.ap[:, q * 256:(q + 1) * 256], sf.ap[:, 0:256], sf.k(), WUKVV.k())
            _, sf = nxt()
            dma("sp", sf.ap[0:33, 0:256], wgla_d[l], W=sf.k())
            cp("pool", WGLA.ap, sf.ap[0:33, 0:256], sf.k(), WGLA.k())

        load_small(0, first=64512, second=66048)
        for l in range(2):
            if l == 0:
                for kc in range(8):
                    if kc % 2 == 0:
                        ts("dve", HT.ap[:, kc, :], XNT.ap[:, kc, :], ACOL.ap[:, l, kc:kc + 1], MODC.ap[:, l, kc:kc + 1],
                           ALU.mult, ALU.add, XNT.k(kc) + ACOL.k() + MODC.k(), HT.k(kc))
                    else:
                        act(HT.ap[:, kc, :], XNT.ap[:, kc, :], AF.Identity, XNT.k(kc) + ACOL.k() + MODC.k(), HT.k(kc),
                            scale=ACOL.ap[:, l, kc:kc + 1], bias=MODC.ap[:, l, kc:kc + 1])
            else:
                prenorm_stats_xn()
                for kc in range(8):
                    b = kc % 2

                    def tr(e, kc=kc, b=b):
                        r = None
                        for t_ in range(8):
                            r = e.transpose(psb[b][:, t_ * 128:(t_ + 1) * 128], MIX.ap[:, t_, kc * 128:(kc + 1) * 128],
                                            IDENT.ap)
                        return r
                    OP("pe", tr, MIX.k() + IDENT.k(), PK(b))
                    ts("dve", HT.ap[:, kc, :], psb[b][:, 0:1024], ACOL.ap[:, l, kc:kc + 1], MODC.ap[:, l, kc:kc + 1],
                       ALU.mult, ALU.add, PK(b) + ACOL.k() + MODC.k(), HT.k(kc) + PK(b))

            mark(2 + 10 * l)

            QTS = AVW(0, [128, 1024], F32)
            KT_ = AVW(4096, [128, 1024], F32)
            KTOK = AVW(8192, [128, 8, 128], F32)
            VTOK = AVW(12288, [128, 8, 256], BF16)
            GAT = AVW(16384, [33, 1024], BF16)
            SP = AVW(18432, [128, 8, 256], F32)
            EB = AVW(26624, [128, 1024], F32)
            ENB = AVW(30720, [128, 1024], F32)
            QBD = AVW(34816, [128, 4, 1024], BF16)
            KTL = AVW(43008, [128, 1024], BF16)
            ED = AVW(45056, [128, 8, 128], F32)
            KHAT = AVW(49152, [128, 8, 128], BF16)
            ATS = AVW(51200, [128, 8, 4, 128], BF16)
            XSD = [AVW(59392, [128, 9, 64], F32), AVW(61696, [128, 9, 64], F32)]
            EBFD = [AVW(64000, [128, 8, 64], BF16), AVW(65024, [128, 8, 64], BF16)]
            DDC = AVW(66048, [128, 16], F32)
            OACC = AVW(66112, [128, 2, 1024], F32)
            KHBD = View(pers, MIX.off + 2 * 2048, [128, 8, 4, 128], BF16, tag="P")
            VBD = AVW(74304, [128, 8, 4, 128], BF16)

            if l == 0:
                wA1 = load_group(l, "A1")
                wA2 = load_group(l, "A2")
            else:
                wA1, wA2 = pre_next

            def evA1(ci, tc, b, w):
                sl = slice(tc * 512, (tc + 1) * 512)
                if ci == 0:
                    act(QTS.ap[:, sl], psum[b][:, 0:512], AF.Copy, PK(b), QTS.k(e=(tc * 512, tc * 512 + 512)) + PK(b),
                        scale=32.0 ** -0.5)
                else:
                    cp("dve", KT_.ap[:, sl], psum[b][:, 0:512], PK(b), KT_.k(e=(tc * 512, tc * 512 + 512)) + PK(b))
            fm_group(wA1, 256, evA1)
            wA3 = load_group(l, "A3")
            memset("pool", GAT.ap[32:33, :], 1.0, GAT.k())
            memset("pool", ATS.ap, 0.0, ATS.k())
            memset("pool", VBD.ap, 0.0, VBD.k())

            def evA2(ci, tc, b, w):
                sl = slice(tc * 512, (tc + 1) * 512)
                cp("dve", GAT.ap[0:32, sl], psum[b][0:32, 0:512], PK(b), GAT.k() + PK(b))
            fm_group(wA2, 32, evA2)
            wA4 = load_group(l, "A4")

            def evA3(ci, tc, b, w):
                sl = slice(tc * 512, (tc + 1) * 512)
                act(MIX.ap[:, ci, sl], psum[b][:, 0:512], AF.Silu, PK(b),
                    MIX.k(e=(ci * 1024 + tc * 512, ci * 1024 + tc * 512 + 512)) + PK(b))
            fm_group(wA3, 256, evA3)
            wA5 = load_group(l, "A5")

            def evA4(t_, b):
                cp("dve", KTOK.ap[:, t_, :], psum[b][:, 0:128], PK(b), KTOK.k(t_) + PK(b))
            tm_group(wA4, 128, evA4)
            wB4 = load_group(l, "B4")

            def evA5(t_, b):
                cp("act", VTOK.ap[:, t_, :], psum[b][:, 0:256], PK(b), VTOK.k(t_) + PK(b))
                for par in range(2):
                    r0 = 64 * par
                    cp("pool", VBD.ap[r0:r0 + 64, t_, :, 64 * par:64 * par + 64],
                       VTOK.ap[r0:r0 + 64, t_, :].rearrange("p (h d) -> p h d", h=4), VTOK.k(t_),
                       VBD.k(t_, p=(r0, r0 + 64)))
            tm_group(wA5, 256, evA5)
            wB1 = load_group(l, "B1")

            mark(31 + 100 * l)
            for t_ in range(8):
                b = next_bank()
                OP("pe", lambda e, t_=t_, b=b: e.matmul(psum[b][:, 0:256], lhsT=GAT.ap[0:33, t_ * 128:(t_ + 1) * 128],
                                                        rhs=WGLA.ap[0:33, 0:256], start=True, stop=True),
                   GAT.k() + WGLA.k(), PK(b))
                act(SP.ap[:, t_, :], psum[b][:, 0:256], AF.Exp, PK(b), SP.k(t_) + PK(b), scale=-1.0)
            act(SP.ap, SP.ap, AF.Ln, SP.k(), SP.k(), bias=ONEF.ap[:, 0:1])

            mark(32 + 100 * l)
            for ph in range(2):
                jobs = [(0, ph), (1, 1 - ph)]

                def J(dr, hf):
                    tiles = list(range(4 * hf, 4 * hf + 4))
                    chunks = list(range(8 * hf, 8 * hf + 8))
                    order = chunks if dr == 0 else chunks[::-1]
                    return dict(dr=dr, hf=hf, tiles=tiles, order=order, ts=slice(512 * hf, 512 * hf + 512),
                                er=(512 * hf, 512 * hf + 512), i0=(8 * hf if dr == 0 else 8 * (1 - hf)))
                JJ = [J(*j) for j in jobs]

                for jb in JJ:
                    dr, hf = jb["dr"], jb["hf"]

                    def mmB(e, dr=dr, hf=hf, tiles=jb["tiles"]):
                        r = None
                        for t_ in tiles:
                            r = e.matmul(psum[4 + hf][:, (t_ % 4) * 128:(t_ % 4 + 1) * 128],
                                         lhsT=SP.ap[:, t_, dr * 128:(dr + 1) * 128], rhs=TRI.ap[:, 2 * dr, :],
                                         start=True, stop=True)
                        return r
                    OP("pe", mmB, SP.k() + TRI.k(), PK(4 + hf))

                    def mmD(e, dr=dr, hf=hf, tiles=jb["tiles"]):
                        r = None
                        for t_ in tiles:
                            r = e.matmul(psum[6 + hf][:, (t_ % 4) * 128:(t_ % 4 + 1) * 128],
                                         lhsT=TRI.ap[:, 2 * dr + 1, :], rhs=SP.ap[:, t_, dr * 128:(dr + 1) * 128],
                                         start=True, stop=True)
                        return r
                    OP("pe", mmD, SP.k() + TRI.k(), PK(6 + hf))
                for jb in JJ:
                    hf, sl, er = jb["hf"], jb["ts"], jb["er"]
                    act(EB.ap[:, sl], psum[4 + hf][:, 0:512], AF.Exp, PK(4 + hf), EB.k(e=er) + PK(4 + hf))
                    act(ENB.ap[:, sl], psum[4 + hf][:, 0:512], AF.Exp, PK(4 + hf), ENB.k(e=er) + PK(4 + hf), scale=-1.0)
                    act(ED.ap[:, 4 * hf:4 * hf + 4, :], psum[6 + hf][:, 0:512].rearrange("p (a b) -> p a b", a=4),
                        AF.Exp, PK(6 + hf), ED.k(4 * hf, 4 * hf + 4) + PK(6 + hf))
                for jb in JJ:
                    dr, hf, sl, er = jb["dr"], jb["hf"], jb["ts"], jb["er"]
                    edge = 63 if dr == 0 else 0
                    tt("dve", DDC.ap[:, 8 * hf:8 * hf + 8], EB.ap[:, 512 * hf + edge:512 * hf + 512:64],
                       KEEPC.ap[:, dr, 8 * hf:8 * hf + 8], ALU.mult, EB.k(e=er) + KEEPC.k(), DDC.k())
                    tt("dve", EB.ap[:, sl], EB.ap[:, sl], QTS.ap[:, sl], ALU.mult, EB.k(e=er) + QTS.k(e=er) + DDC.k(), EB.k(e=er))
                    for h_ in range(4):
                        act(QBD.ap[:, h_, sl], EB.ap[:, sl], AF.Copy, EB.k(e=er) + HMASK.k(),
                            QBD.k(e=(h_ * 1024 + er[0], h_ * 1024 + er[1])), scale=HMASK.ap[:, h_:h_ + 1])
                    tt("dve", KTL.ap[:, sl], KT_.ap[:, sl], ENB.ap[:, sl], ALU.mult, KT_.k(e=er) + ENB.k(e=er), KTL.k(e=er))
                for jb in JJ:
                    hf = jb["hf"]
                    t0, t1 = 4 * hf, 4 * hf + 4
                    tt("dve", KHAT.ap[:, t0:t1, :], KTOK.ap[:, t0:t1, :], ED.ap[:, t0:t1, :], ALU.mult,
                       KTOK.k(t0, t1) + ED.k(t0, t1), KHAT.k(t0, t1))
                    for h_ in range(4):
                        tt("dve" if h_ < 2 else "pool", KHBD.ap[:, t0:t1, h_, :], KHAT.ap[:, t0:t1, :],
                           HMF.ap[:, h_, :].unsqueeze(1).broadcast_to([128, 4, 128]), ALU.mult,
                           KHAT.k(t0, t1) + HMF.k(), KHBD.k(t0, t1))
                for gi in range(2):
                    for jb in JJ:
                        dr, hf = jb["dr"], jb["hf"]
                        b = hf if gi == 0 else 4 + hf
                        pr0 = 4 * hf + 2 * gi

                        def mmA(e, pr0=pr0, b=b):
                            r = None
                            for pi in range(2):
                                pr = pr0 + pi
                                for par in range(2):
                                    c = 2 * pr + par
                                    r = e.matmul(psum[b][64 * par:64 * par + 64, pi * 256:(pi + 1) * 256],
                                                 lhsT=KTL.ap[:, c * 64:(c + 1) * 64],
                                                 rhs=QBD.ap[:, :, c * 64:(c + 1) * 64], start=True, stop=True)
                            return r
                        OP("pe", mmA, KTL.k(e=jb["er"]) + QBD.k(), PK(b))
                        for par in range(2):
                            r0 = 64 * par
                            tt("dve", ATS.ap[r0:r0 + 64, pr0:pr0 + 2, :, 64 * par:64 * par + 64],
                               psum[b][r0:r0 + 64, 0:512].rearrange("p (a h i) -> p a h i", a=2, h=4),
                               AMASK.ap[r0:r0 + 64, dr, :].unsqueeze(1).unsqueeze(1).broadcast_to([64, 2, 4, 64]), ALU.mult,
                               PK(b) + AMASK.k(), ATS.k(pr0, pr0 + 2, p=(r0, r0 + 64)) + PK(b))
                for jb in JJ:
                    hf = jb["hf"]

                    def mmU(e, tiles=jb["tiles"], hf=hf):
                        r = None
                        for pr in tiles:
                            for h_ in range(4):
                                r = e.matmul(psum[2 + hf][:, (pr % 4) * 128:(pr % 4 + 1) * 128],
                                             lhsT=KHBD.ap[:, pr, h_, :], rhs=VBD.ap[:, pr, h_, :],
                                             start=(h_ == 0), stop=(h_ == 3))
                        return r
                    OP("pe", mmU, KHBD.k(4 * hf, 4 * hf + 4) + VBD.k(4 * hf, 4 * hf + 4), PK(2 + hf))
                for jb in JJ:
                    dr = jb["dr"]
                    XSj = XSD[dr]
                    if ph == 0:
                        dma("sp", XSj.ap[:, 0, :], sgla_d[l, dr], W=XSj.k(0))
                    else:
                        cp("pool", XSj.ap[:, 0, :], XSj.ap[:, 8, :], XSj.k(8), XSj.k(0))
                for j in range(8):
                    for jb in JJ:
                        dr, hf = jb["dr"], jb["hf"]
                        XSj = XSD[dr]
                        c = jb["order"][j]
                        stt("dve", XSj.ap[:, j + 1, :], XSj.ap[:, j, :], DDC.ap[:, c:c + 1],
                            psum[2 + hf][:, (c % 8) * 64:(c % 8 + 1) * 64], ALU.mult, ALU.add,
                            XSj.k(j) + DDC.k() + PK(2 + hf), XSj.k(j + 1) + PK(2 + hf))
                for jb in JJ:
                    dr, hf, i0 = jb["dr"], jb["hf"], jb["i0"]
                    XSj = XSD[dr]
                    tt("dve", EBFD[dr].ap, XSj.ap[:, 0:8, :],
                       KEEPP.ap[:, dr, i0:i0 + 8].unsqueeze(2).broadcast_to([128, 8, 64]),
                       ALU.mult, XSj.k() + KEEPP.k(), EBFD[dr].k())
                    for s_ in (2 * hf, 2 * hf + 1):
                        slot = 4 * (s_ % 2) + 4 if dr == 0 else 8 - 4 * (s_ % 2)
                        dma("sp", ogla_d[l, dr, s_], XSj.ap[:, slot, :], R=XSj.k(slot))
                for jb in JJ:
                    dr, hf, sl = jb["dr"], jb["hf"], jb["ts"]
                    pos = {c: j for j, c in enumerate(jb["order"])}

                    def mmO(e, pos=pos, dr=dr, hf=hf, tiles=jb["tiles"]):
                        r = None
                        for pr in tiles:
                            for h_ in range(4):
                                bank = 4 + (h_ // 2) * 2 + hf
                                col0 = (pr % 4) * 128
                                o = psum[bank][64 * (h_ % 2):64 * (h_ % 2) + 64, col0:col0 + 128]
                                e.matmul(o, lhsT=VTOK.ap[:, pr, 64 * h_:64 * h_ + 64], rhs=ATS.ap[:, pr, h_, :],
                                         start=True, stop=False)
                                for par in range(2):
                                    c = 2 * pr + par
                                    o2 = psum[bank][64 * (h_ % 2):64 * (h_ % 2) + 64, col0 + 64 * par:col0 + 64 * par + 64]
                                    r = e.matmul(o2, lhsT=EBFD[dr].ap[:, pos[c], :], rhs=QBD.ap[:, h_, c * 64:(c + 1) * 64],
                                                 start=False, stop=(par == 1))
                        return r
                    OP("pe", mmO, VTOK.k(4 * hf, 4 * hf + 4) + ATS.k(4 * hf, 4 * hf + 4) + EBFD[dr].k() + QBD.k(),
                       PK(4 + hf, 6 + hf))
                    for ch in range(2):
                        bank = 4 + ch * 2 + hf
                        ke = OACC.k(e=(ch * 1024 + 512 * hf, ch * 1024 + 512 * hf + 512))
                        if ph == 0:
                            cp("act", OACC.ap[:, ch, sl], psum[bank][:, 0:512], PK(bank), ke + PK(bank))
                        else:
                            tt("dve", OACC.ap[:, ch, sl], OACC.ap[:, ch, sl], psum[bank][:, 0:512], ALU.add,
                               PK(bank) + ke, ke + PK(bank))

            mark(38 + 100 * l)
            SQ = AVW(34816, [128, 2, 1024], BF16)
            RS_ = AVW(26624, [128, 2, 1024], F32)
            act(SQ.ap, OACC.ap, AF.Square, OACC.k(), SQ.k())
            for ch in range(2):
                for tc in range(2):
                    b = next_bank(0, 4)
                    sl = slice(tc * 512, (tc + 1) * 512)
                    OP("pe", lambda e, ch=ch, sl=sl, b=b: e.matmul(psum[b][:, 0:512], lhsT=ONESBD.ap, rhs=SQ.ap[:, ch, sl],
                                                                  start=True, stop=True),
                       SQ.k() + ONESBD.k(), PK(b))
                    ke = RS_.k(e=(ch * 1024 + tc * 512, ch * 1024 + tc * 512 + 512))
                    act(RS_.ap[:, ch, sl], psum[b][:, 0:512], AF.Ln, PK(b), ke + PK(b), scale=1.0 / 64, bias=EPSB.ap[:, 0:1])
                    act(RS_.ap[:, ch, sl], RS_.ap[:, ch, sl], AF.Exp, ke, ke, scale=-0.5)
                    ko = OACC.k(e=(ch * 1024 + tc * 512, ch * 1024 + tc * 512 + 512))
                    tt("dve", OACC.ap[:, ch, sl], OACC.ap[:, ch, sl], RS_.ap[:, ch, sl], ALU.mult, ko + ke, ko)
                    km = MIX.k(e=(ch * 1024 + tc * 512, ch * 1024 + tc * 512 + 512))
                    stt("dve", MIX.ap[:, ch, sl], OACC.ap[:, ch, sl], GGLA.ap[:, l:l + 1], MIX.ap[:, ch, sl],
                        ALU.mult, ALU.mult, ko + km + GGLA.k(), km)

            mark(3 + 10 * l)
            CQG = AVW(0, [128, 2, 1024], BF16)
            SQT = AVW(4096, [128, 2, 1024], BF16)
            RBC = AVW(8192, [128, 1024], F32)
            RC = AVW(12288, [128, 1024], F32)
            RSN = AVW(16384, [128, 1024], F32)
            CKV = AVW(20480, [128, 8, 128], F32)
            CKVB = AVW(24576, [128, 12, 128], BF16)
            CKVT = AVW(27648, [128, 1536], BF16)
            KPE = AVW(30720, [128, 8, 32], F32)
            KPT = AVW(31744, [128, 8, 32], F32)
            KRB = AVW(32768, [128, 12, 32], BF16)
            KPET = AVW(33536, [32, 1536], BF16)
            KPET128 = AVW(33536, [128, 1536], BF16)
            CST1 = AVW(36608, [128, 4, 128], F32)
            CST2 = AVW(38656, [128, 4, 32], F32)
            AQ = AVW(39168, [128, 4, 1024], BF16)
            AK = AVW(47360, [128, 4, 1536], BF16)
            AVM = AVW(59648, [128, 12, 2, 192], BF16)
            QT1s = [AVW(4096, [128, 512], F32), AVW(79872, [128, 512], F32)]
            QT2s = [AVW(6144, [128, 512], F32), AVW(81920, [128, 512], F32)]
            qtc = [0]
            OT = AVW(36608, [128, 512], F32)

            memset("pool", KPET128.ap, 0.0, KPET128.k())

            SSK = SMALL.ap[:, 8:16]

            def evB4(t_, b):
                cp("dve", CKV.ap[:, t_, :], psum[b][:, 0:128], PK(b), CKV.k(t_) + PK(b))
                act(KPT.ap[:, 0:4, :].rearrange("p a b -> p (a b)"), psum[b][:, 0:128], AF.Square, PK(b),
                    KPT.k() + SMALL.k() + PK(b), accum=SMALL.ap[:, 8 + t_:9 + t_])
                cp("dve", KPE.ap[:, t_, :], psum[b][:, 128:160], PK(b), KPE.k(t_) + PK(b))
            tm_group(wB4, 160, evB4)
            wB2 = load_group(l, "B2")
            dma("sp", okpe_d[l].rearrange("(t p) d -> p t d", p=128), KPE.ap, R=KPE.k())

            rstd_inplace(SSK, SMALL.k(), 1.0 / 128)
            tt("dve", CKV.ap, CKV.ap, SSK.unsqueeze(2).broadcast_to([128, 8, 128]), ALU.mult, CKV.k() + SMALL.k(), CKV.k())
            tt("dve", CKV.ap, CKV.ap, GMKV.ap[:, l, :].unsqueeze(1).broadcast_to([128, 8, 128]), ALU.mult,
               CKV.k() + GMKV.k(), CKV.k())
            dma("sp", ockv_d[l].rearrange("(t p) d -> p t d", p=128), CKV.ap, R=CKV.k())
            cp("act", CKVB.ap[:, 0:8, :], CKV.ap, CKV.k(), CKVB.k(0, 8))
            dma("sp", CST1.ap, cckv_d[l], W=CST1.k())
            cp("act", CKVB.ap[:, 8:12, :], CST1.ap, CST1.k(), CKVB.k(8, 12))
            dma("sp", CST2.ap, ckpe_d[l], W=CST2.k())
            cp("act", KRB.ap[:, 8:12, :], CST2.ap, CST2.k(), KRB.k(8, 12))
            kv4 = KPE.ap.rearrange("p t (g s e) -> p t g s e", g=2, s=2)
            tv4 = KPT.ap.rearrange("p t (g s e) -> p t g s e", g=2, s=2)
            sn4 = ROPET.ap[:, :, 1, :].rearrange("p t (g s e) -> p t g s e", g=2, s=2)
            for s_ in range(2):
                for g_ in range(2):
                    tt("dve", tv4[:, :, g_, s_, :], kv4[:, :, g_, 1 - s_, :], sn4[:, :, g_, s_, :], ALU.mult,
                       KPE.k() + ROPET.k(), KPT.k())
            tt("dve", KPE.ap, KPE.ap, ROPET.ap[:, :, 0, :], ALU.mult, KPE.k() + ROPET.k(), KPE.k())
            tt("dve", KRB.ap[:, 0:8, :], KPE.ap, KPT.ap, ALU.add, KPE.k() + KPT.k(), KRB.k(0, 8))

            def evB1(ci, tc, b, w):
                sl = slice(tc * 512, (tc + 1) * 512)
                ke = dict(e=(ci * 1024 + tc * 512, ci * 1024 + tc * 512 + 512))
                ts("dve", CQG.ap[:, ci, sl], psum[b][:, 0:512], GMQ.ap[:, l, ci:ci + 1], None, ALU.mult, None,
                   PK(b) + GMQ.k(), CQG.k(**ke) + PK(b))
                act(SQT.ap[:, ci, sl], psum[b][:, 0:512], AF.Square, PK(b), SQT.k(**ke) + PK(b))
            fm_group(wB1, 256, evB1)
            wB3 = load_group(l, "B3")

            for hb in range(2):
                b = next_bank()
                nt = 8 if hb == 0 else 4

                def trc(e, hb=hb, b=b, nt=nt):
                    r = None
                    for i in range(nt):
                        r = e.transpose(psb[b][:, i * 128:(i + 1) * 128], CKVB.ap[:, hb * 8 + i, :], IDENT.ap)
                    return r
                OP("pe", trc, CKVB.k() + IDENT.k(), PK(b))
                cp("act", CKVT.ap[:, hb * 1024:hb * 1024 + nt * 128], psb[b][:, 0:nt * 128], PK(b),
                   CKVT.k(e=(hb * 1024, hb * 1024 + nt * 128)) + PK(b))
                b2 = next_bank()

                def trk(e, hb=hb, b2=b2, nt=nt):
                    r = None
                    for i in range(nt):
                        r = e.transpose(psb[b2][0:32, i * 128:(i + 1) * 128], KRB.ap[:, hb * 8 + i, :], IDENT.ap)
                    return r
                OP("pe", trk, KRB.k() + IDENT.k(), PK(b2))
                cp("dve", KPET.ap[:, hb * 1024:hb * 1024 + nt * 128], psb[b2][0:32, 0:nt * 128], PK(b2),
                   KPET.k(e=(hb * 1024, hb * 1024 + nt * 128)) + PK(b2))

            for tc in range(2):
                b = next_bank()
                sl = slice(tc * 512, (tc + 1) * 512)

                def mmss(e, sl=sl, b=b):
                    e.matmul(psum[b][:, 0:512], lhsT=ONES.ap, rhs=SQT.ap[:, 0, sl], start=True, stop=False)
                    return e.matmul(psum[b][:, 0:512], lhsT=ONES.ap, rhs=SQT.ap[:, 1, sl], start=False, stop=True)
                OP("pe", mmss, SQT.k() + ONES.k(), PK(b))
                ke = RBC.k(e=(tc * 512, tc * 512 + 512))
                act(RBC.ap[:, sl], psum[b][:, 0:512], AF.Ln, PK(b), ke + PK(b), scale=1.0 / 256, bias=EPSB.ap[:, 0:1])
                act(RBC.ap[:, sl], RBC.ap[:, sl], AF.Exp, ke, ke, scale=-0.5)
            for ci in range(2):
                tt("dve", CQG.ap[:, ci, :], CQG.ap[:, ci, :], RBC.ap, ALU.mult, CQG.k(ci) + RBC.k(), CQG.k(ci))

            def evB23(base):
                def ev(ci, tc, b, w):
                    sl = slice(tc * 512, (tc + 1) * 512)
                    ch = base + ci
                    act(MIX.ap[:, ch, sl], psum[b][:, 0:512], AF.Silu, PK(b),
                        MIX.k(e=(ch * 1024 + tc * 512, ch * 1024 + tc * 512 + 512)) + PK(b))
                return ev
            fm_group(wB2, 256, evB23(2))
            wC1 = load_group(l, "C1")
            fm_group(wB3, 256, evB23(4))
            wC2 = load_group(l, "C2")

            for j in range(4):
                dma("sp", AQ.ap[96:101, j, :], maskq_d, W=AQ.k(j))
                dma("sp", AK.ap[96:101, j, :], maskk_d, W=AK.k(j))
            memset("pool", AVM.ap[:, :, :, 64:128], 1.0, AVM.k())

            for hh in range(2):
                for kt in range(12):
                    b = next_bank(0, 6)
                    OP("pe", lambda e, kt=kt, b=b, hh=hh: e.matmul(psum[b][:, 0:256], lhsT=CKVT.ap[:, kt * 128:(kt + 1) * 128],
                                                                  rhs=WUKVV.ap[:, hh * 256:(hh + 1) * 256], start=True, stop=True),
                       CKVT.k() + WUKVV.k(), PK(b))
                    dst = AVM.ap[:, kt, :, :].rearrange("p a (s d) -> p a s d", s=3)[:, :, 0:3:2, :]
                    cp("act", dst, psum[b][:, 0:256].rearrange("p (a s d) -> p a s d", a=2, s=2),
                       PK(b), AVM.k(kt) + PK(b))
                for j in range(4):
                    h_ = 4 * hh + j
                    for tc in range(2):
                        sl = slice(tc * 512, (tc + 1) * 512)
                        bA = next_bank(0, 6)

                        def mmQ(e, h_=h_, sl=sl, bA=bA):
                            r = None
                            for ci in range(2):
                                r = e.matmul(psum[bA][:, 0:512], lhsT=WUQ.ap[:, ci, 128 * h_:128 * h_ + 128],
                                             rhs=CQG.ap[:, ci, sl], start=(ci == 0), stop=(ci == 1))
                            return r
                        OP("pe", mmQ, WUQ.k() + CQG.k(), PK(bA))
                        QT1, QT2 = QT1s[qtc[0] % 2], QT2s[qtc[0] % 2]
                        qtc[0] += 1
                        er = (j * 1024 + tc * 512, j * 1024 + tc * 512 + 512)
                        cp("act", AQ.ap[0:64, j, sl], psum[bA][0:64, 0:512], PK(bA), AQ.k(e=er, p=(0, 64)) + PK(bA))
                        tt("dve", QT1.ap[64:96, :], psum[bA][64:96, 0:512], ROPEF.ap[64:96, 0, sl], ALU.mult,
                           PK(bA) + ROPEF.k(), QT1.k() + PK(bA))
                        tt("dve", QT2.ap[64:96, :], psum[bA][96:128, 0:512], ROPEF.ap[96:128, 1, sl], ALU.mult,
                           PK(bA) + ROPEF.k(), QT2.k() + PK(bA))
                        tt("pool", AQ.ap[64:96, j, sl], QT1.ap[64:96, :], QT2.ap[64:96, :], ALU.add,
                           QT1.k() + QT2.k(), AQ.k(e=er, p=(64, 96)))
                    for k3 in range(3):
                        sl = slice(k3 * 512, (k3 + 1) * 512)
                        b = next_bank(0, 6)

                        def mmK(e, h_=h_, sl=sl, b=b):
                            e.matmul(psum[b][0:96, 0:512], lhsT=WUKVK.ap[:, 96 * h_:96 * h_ + 96], rhs=CKVT.ap[:, sl],
                                     start=True, stop=False)
                            return e.matmul(psum[b][0:96, 0:512], lhsT=IPAD.ap[:, 0:96], rhs=KPET128.ap[:, sl],
                                            start=False, stop=True)
                        OP("pe", mmK, WUKVK.k() + CKVT.k() + IPAD.k() + KPET128.k(), PK(b))
                        cp("act", AK.ap[0:96, j, sl], psum[b][0:96, 0:512], PK(b),
                           AK.k(e=(j * 1536 + k3 * 512, j * 1536 + k3 * 512 + 512), p=(0, 96)) + PK(b))

                def vl_mla(tag, kt):
                    jj = tag
                    pair, odd = jj // 2, jj % 2
                    ap = AVM.ap[:, kt, pair, 64 * odd:64 * odd + 128]
                    return ap, AVM.k(kt)

                def fin_mla(blk, tc, ob, tag, hh=hh):
                    h_ = 4 * hh + blk
                    odd = h_ % 2
                    dlo, slo = (0, 64) if odd == 0 else (64, 0)
                    ch = 2 + h_ // 2
                    sl = slice(tc * 512, (tc + 1) * 512)
                    OP("dve", lambda e, ob=ob, slo=slo: e.reciprocal(out=RBUF.ap[slo:slo + 64, 0, :],
                                                                   in_=psum[ob][slo:slo + 64, 0:512]),
                       PK(ob), RBUF.k(0) + PK(ob))
                    tt("dve", OT.ap[dlo:dlo + 64, :], psum[ob][dlo:dlo + 64, 0:512], RBUF.ap[slo:slo + 64, 0, :], ALU.mult,
                       PK(ob) + RBUF.k(0), OT.k() + PK(ob))
                    km = MIX.k(e=(ch * 1024 + tc * 512, ch * 1024 + tc * 512 + 512), p=(dlo, dlo + 64))
                    tt("pool", MIX.ap[dlo:dlo + 64, ch, sl], MIX.ap[dlo:dlo + 64, ch, sl], OT.ap[dlo:dlo + 64, :], ALU.mult,
                       km + OT.k(), km)

                items = []
                idx = 0
                for j in range(4):
                    for tc in range(2):
                        items.append((j, tc, 6 + idx % 2, j))
                        idx += 1
                attention_core(items, AQ, AK, 101, 96.0 ** -0.5, vl_mla, fin_mla, PTB, nsb=6)

            mark(4 + 10 * l)
            DQR = AVW(0, [128, 2, 1024], BF16)
            DKR = AVW(4096, [128, 2, 1536], BF16)
            DT1s = [AVW(10240, [128, 512], F32), AVW(14336, [128, 512], F32)]
            DT2s = [AVW(12288, [128, 512], F32), AVW(16384, [128, 512], F32)]
            dtc = [0]
            CDK = AVW(14336, [128, 4, 256], F32)
            CDKB = AVW(18432, [128, 4, 256], BF16)
            CDV = AVW(20480, [128, 4, 256], F32)
            AVD = AVW(24576, [128, 12, 2, 192], BF16)
            OSTG = [AVW(33792 + i * 1024, [128, 256], F32) for i in range(4)]
            AQD = AVW(37888, [128, 2, 1024], BF16)
            AKD = AVW(41984, [128, 2, 1536], BF16)
            OCB = AVW(48128, [128, 1024], F32)
            OCA = AVW(52224, [128, 512], F32)
            OCT = AVW(54272, [128, 512], F32)
            SQD = AVW(56320, [128, 512], BF16)
            RSD = AVW(57344, [128, 512], F32)
            RBUFD = AVW(14336, [128, 2, 512], F32)
            PTD = [AVW(20480 + i * 1024, [128, 512], BF16) for i in range(4)]
            SQD2 = SQD

            def evC12(g):
                def ev(ci, tc, b, w):
                    sl = slice(tc * 512, (tc + 1) * 512)
                    DT1, DT2 = DT1s[dtc[0] % 2], DT2s[dtc[0] % 2]
                    if ci == 0:
                        tt("dve", DT1.ap, psum[b][:, 0:512], ROPEF.ap[:, 0, sl], ALU.mult, PK(b) + ROPEF.k(), DT1.k() + PK(b))
                    else:
                        tt("dve", DT2.ap, psum[b][:, 0:512], ROPEF.ap[:, 1, sl], ALU.mult, PK(b) + ROPEF.k(), DT2.k() + PK(b))
                        tt("pool", DQR.ap[:, g, sl], DT1.ap, DT2.ap, ALU.add, DT1.k() + DT2.k(),
                           DQR.k(e=(g * 1024 + tc * 512, g * 1024 + tc * 512 + 512)))
                        dtc[0] += 1
                return ev

            def evC34(g):
                def ev(ci, tc, b, w):
                    sl = slice(tc * 512, (tc + 1) * 512)
                    DT1, DT2 = DT1s[dtc[0] % 2], DT2s[dtc[0] % 2]
                    if ci == 0:
                        tt("dve", DT1.ap, psum[b][:, 0:512], ROPEF.ap[:, 0, sl], ALU.mult, PK(b) + ROPEF.k(), DT1.k() + PK(b))
                    else:
                        tt("dve", DT2.ap, psum[b][:, 0:512], ROPEF.ap[:, 1, sl], ALU.mult, PK(b) + ROPEF.k(), DT2.k() + PK(b))
                        tt("pool", DKR.ap[:, g, sl], DT1.ap, DT2.ap, ALU.add, DT1.k() + DT2.k(),
                           DKR.k(e=(g * 1536 + tc * 512, g * 1536 + tc * 512 + 512)))
                        dtc[0] += 1
                return ev

            def fm_group_pair(wb, evac):
                for tc in range(2):
                    for ci in range(2):
                        b = next_bank()

                        def mm(e, wb=wb, ci=ci, tc=tc, b=b):
                            r = None
                            for kc in range(8):
                                r = e.matmul(psum[b][:, 0:512], lhsT=wb.ap[:, kc, ci * 128:ci * 128 + 128],
                                             rhs=HT.ap[:, kc, tc * 512:(tc + 1) * 512], start=(kc == 0), stop=(kc == 7))
                            return r
                        OP("pe", mm, wb.k() + HT.k(), PK(b))
                        evac(ci, tc, b, 128)

            WOUT = AVW(59392, [128, 8, 1024], BF16)
            AQS = [AQD, AVW(33792, [128, 2, 1024], BF16)]
            AKS = [AKD, AVW(75776, [128, 2, 1536], BF16)]

            dma("sp", CDK.ap, cdk_d[l], W=CDK.k())
            dma("sp", CDV.ap, cdv_d[l], W=CDV.k())
            cp("dve", CDKB.ap, CDK.ap, CDK.k(), CDKB.k())
            bt = next_bank()

            def trdk(e, bt=bt):
                r = None
                for g in range(2):
                    for kt in range(4):
                        r = e.transpose(psb[bt][:, (g * 4 + kt) * 128:(g * 4 + kt + 1) * 128],
                                        CDKB.ap[:, kt, g * 128:(g + 1) * 128], IDENT.ap)
                return r
            OP("pe", trdk, CDKB.k() + IDENT.k(), PK(bt))
            for g in range(2):
                cp("dve", DKR.ap[:, g, 1024:1536], psb[bt][:, g * 512:(g + 1) * 512], PK(bt),
                   DKR.k(e=(g * 1536 + 1024, g * 1536 + 1536)) + PK(bt))

            def prep_set(si):
                memset("dve" if si == 0 else "pool", AQS[si].ap, 0.0, AQS[si].k())
                memset("pool" if si == 0 else "dve", AKS[si].ap, 0.0, AKS[si].k())
                for c_ in range(2):
                    dma("sp", AQS[si].ap[32:37, c_, :], maskq_d, W=AQS[si].k(c_))
                    dma("sp", AKS[si].ap[32:37, c_, :], maskk_d, W=AKS[si].k(c_))

            def relayout(h_):
                si = h_ % 2
                odd, g = h_ % 2, h_ // 2
                for c_ in range(2):
                    r0 = odd * 64 + c_ * 32
                    dma("sp", AQS[si].ap[0:32, c_, :], DQR.ap[r0:r0 + 32, g, :], R=DQR.k(g), W=AQS[si].k(c_))
                    dma("sp", AKS[si].ap[0:32, c_, :], DKR.ap[r0:r0 + 32, g, :], R=DKR.k(g), W=AKS[si].k(c_))

            prep_set(0)

            fm_group_pair(wC1, evC12(0))
            wC3 = load_group(l, "C3")
            fm_group_pair(wC2, evC12(1))
            wC4 = load_group(l, "C4")
            fm_group_pair(wC3, evC34(0))
            wC7 = load_group(l, "C7")
            fm_group_pair(wC4, evC34(1))
            relayout(0)
            wC5 = load_group(l, "C5")

            ostg_i = [0]
            memset("pool", AVD.ap[:, :, :, 64:128], 1.0, AVD.k())
            dstc = AVD.ap[:, 8:12, :, :].rearrange("p t a (s d) -> p t a s d", s=3)[:, :, :, 0:3:2, :]
            cp("act", dstc, CDV.ap.rearrange("p t (a s d) -> p t a s d", a=2, s=2), CDV.k(), AVD.k(8, 12))

            def evC7(t_, b):
                s = OSTG[ostg_i[0] % 4]
                ostg_i[0] += 1
                cp("dve", s.ap, psum[b][:, 0:256], PK(b), s.k() + PK(b))
                dma("sp", odv_d[l, t_ * 128:(t_ + 1) * 128, :], s.ap, R=s.k())
                dst = AVD.ap[:, t_, :, :].rearrange("p a (s d) -> p a s d", s=3)[:, :, 0:3:2, :]
                cp("act", dst, psum[b][:, 0:256].rearrange("p (a s d) -> p a s d", a=2, s=2), PK(b), AVD.k(t_) + PK(b))
            tm_group(wC7, 256, evC7)
            wC6 = load_group(l, "C6")

            def evC5(ci, tc, b, w):
                sl = slice(tc * 512, (tc + 1) * 512)
                ch = 6 + ci
                act(MIX.ap[:, ch, sl], psum[b][:, 0:512], AF.Silu, PK(b),
                    MIX.k(e=(ch * 1024 + tc * 512, ch * 1024 + tc * 512 + 512)) + PK(b))
            fm_group(wC5, 256, evC5)

            def evC6(t_, b):
                s = OSTG[ostg_i[0] % 4]
                ostg_i[0] += 1
                cp("dve", s.ap, psum[b][:, 0:256], PK(b), s.k() + PK(b))
                dma("sp", odk_d[l, t_ * 128:(t_ + 1) * 128, :], s.ap, R=s.k())
            tm_group(wC6, 256, evC6)

            prep_set(1)
            relayout(1)

            def wout_dma(g4):
                s = wslot[0] % 2
                wslot[0] += 1
                dma("sp", WST[s].ap, wout_d[l, :, g4 * 256:(g4 + 1) * 256].rearrange("(k p) c -> p k c", p=128), W=WST[s].k())
                return (s, g4)

            def wout_cast(h):
                s, g4 = h
                wk = [("WOUTg", g4)] + (WOUT.k() if g4 == 0 else [])
                rk = WST[s].k() + ([] if g4 == 0 else WOUT.k())
                cp("dve" if g4 % 2 == 0 else "pool", WOUT.ap[:, :, g4 * 256:(g4 + 1) * 256], WST[s].ap, rk, wk)

            pend = [wout_dma(0), wout_dma(1)]

            for h_ in range(4):
                odd = h_ % 2
                g = h_ // 2

                def vl_d(tag, kt, h_=h_):
                    pair, od = h_ // 2, h_ % 2
                    return AVD.ap[:, kt, pair, 64 * od:64 * od + 128], AVD.k(kt)

                def fin_d(blk, tc, ob, tag, h_=h_):
                    od = h_ % 2
                    dlo, slo = (0, 64) if od == 0 else (64, 0)
                    pair = h_ // 2
                    sl = slice(tc * 512, (tc + 1) * 512)
                    OP("dve", lambda e, ob=ob, slo=slo, blk=blk: e.reciprocal(out=RBUFD.ap[slo:slo + 64, blk, :],
                                                                            in_=psum[ob][slo:slo + 64, 0:512]),
                       PK(ob), RBUFD.k(blk) + PK(ob))
                    dst = OCA if blk == 0 else OCT
                    tt("dve", dst.ap[dlo:dlo + 64, :], psum[ob][dlo:dlo + 64, 0:512], RBUFD.ap[slo:slo + 64, blk, :], ALU.mult,
                       PK(ob) + RBUFD.k(blk), dst.k() + PK(ob))
                    if blk == 1:
                        ko = OCB.k(e=(tc * 512, tc * 512 + 512), p=(dlo, dlo + 64))
                        stt("dve", OCB.ap[dlo:dlo + 64, sl], OCT.ap[dlo:dlo + 64, :], NEGLAM.ap[dlo:dlo + 64, l:l + 1],
                            OCA.ap[dlo:dlo + 64, :], ALU.mult, ALU.add, OCT.k() + OCA.k() + NEGLAM.k(), ko)
                        if od == 1:
                            def tail(tc=tc, sl=sl, pair=pair):
                                b = 5
                                kob = OCB.k(e=(tc * 512, tc * 512 + 512))
                                act(SQD.ap, OCB.ap[:, sl], AF.Square, kob, SQD.k())
                                OP("pe", lambda e, b=b: e.matmul(psum[b][:, 0:512], lhsT=ONESBD.ap, rhs=SQD.ap, start=True, stop=True),
                                   SQD.k() + ONESBD.k(), PK(b))
                                act(RSD.ap, psum[b][:, 0:512], AF.Ln, PK(b), RSD.k() + PK(b), scale=1.0 / 64, bias=EPSB.ap[:, 0:1])
                                act(RSD.ap, RSD.ap, AF.Exp, RSD.k(), RSD.k(), scale=-0.5)
                                tt("dve", OCB.ap[:, sl], OCB.ap[:, sl], RSD.ap, ALU.mult, kob + RSD.k(), kob)
                                ch = 6 + pair
                                km = MIX.k(e=(ch * 1024 + tc * 512, ch * 1024 + tc * 512 + 512))
                                stt("dve", MIX.ap[:, ch, sl], OCB.ap[:, sl], GDC.ap[:, l:l + 1], MIX.ap[:, ch, sl],
                                    ALU.mult, ALU.mult, kob + km + GDC.k(), km)
                            deferred.append([10, tail])

                items = []
                for tc in range(2):
                    for c_ in range(2):
                        items.append((c_, tc, 6 + c_, c_))
                attention_core(items, AQS[h_ % 2], AKS[h_ % 2], 128, 32.0 ** -0.5, vl_d, fin_d, PTD)
                if h_ + 2 < 4:
                    relayout(h_ + 2)
                if h_ == 0:
                    wout_cast(pend[0]); wout_cast(pend[1])
                    pend = [wout_dma(2), wout_dma(3)]
                    if l == 0:
                        load_small(1, first=82432, second=18432)
                elif h_ == 1:
                    wout_cast(pend[0]); wout_cast(pend[1])
                    if l == 0:
                        pend = [load_dma(1, "A1"), load_dma(1, "A2")]
                elif h_ == 2:
                    if l == 0:
                        pre_next = (load_cast(pend[0], eng="dve"), load_cast(pend[1], eng="pool"))

            run_deferred(force=True)
            mark(5 + 10 * l)
            if DEBUG:
                dma("sp", dbgmix_d[l], MIX.ap, R=MIX.k())
            YT = [AVW(0 + i * 2048, [128, 512], F32) for i in range(4)]
            SQJ = [AVW(8192 + i * 1024, [128, 512], BF16) for i in range(2)]
            for t_ in range(8):
                bs = (next_bank(0, 6), next_bank(0, 6))
                kss = [("opss", l, t_)]
                for nb in range(2):
                    b = bs[nb]

                    def mmo(e, t_=t_, nb=nb, b=b):
                        r = None
                        for kc in range(8):
                            r = e.matmul(psum[b][:, 0:512], lhsT=MIX.ap[:, kc, t_ * 128:(t_ + 1) * 128],
                                         rhs=WOUT.ap[:, kc, nb * 512:(nb + 1) * 512], start=(kc == 0), stop=(kc == 7))
                        return r
                    OP("pe", mmo, MIX.k() + WOUT.k() + [("WOUTg", g_) for g_ in range(4)], PK(b))
                    sq = SQJ[nb]
                    act(sq.ap, psum[b][:, 0:512], AF.Square, PK(b) + SMALL.k(), sq.k() + [("opss", l, t_, nb)] + PK(b),
                        accum=SMALL.ap[:, 16 + 2 * t_ + nb:17 + 2 * t_ + nb])
                s0 = SMALL.ap[:, 16 + 2 * t_:17 + 2 * t_]
                s1 = SMALL.ap[:, 17 + 2 * t_:18 + 2 * t_]
                tt("dve", s0, s0, s1, ALU.add, [("opss", l, t_, 0), ("opss", l, t_, 1)] + SMALL.k(), kss)
                act(s0, s0, AF.Ln, kss + SMALL.k(), kss, scale=1.0 / D, bias=EPSB.ap[:, 0:1])
                act(s0, s0, AF.Exp, kss + SMALL.k(), kss, scale=-0.5)
                for nb in range(2):
                    b = bs[nb]
                    y = YT[(2 * t_ + nb) % 4]
                    sl = slice(nb * 512, (nb + 1) * 512)
                    stt("dve", y.ap, psum[b][:, 0:512], s0, GGB[l].ap[:, sl], ALU.mult, ALU.mult,
                        PK(b) + kss + SMALL.k() + GGB[l].k(), y.k() + PK(b))
                    kx = X.k(e=(t_ * 1024 + nb * 512, t_ * 1024 + nb * 512 + 512))
                    tt("pool" if nb else "dve", X.ap[:, t_, sl], X.ap[:, t_, sl], y.ap, ALU.add, kx + y.k(), kx)
                if l == 1:
                    dma("sp", y_d[t_ * 128:(t_ + 1) * 128, :], X.ap[:, t_, :], R=X.k(t_))
            mark(6 + 10 * l)

        P.emit(nc, st, final_eng="sp")
    return nc


_IN_SIZES = (128, 128, 256, 32, 256, 256, 128, 32, 512, 256, 256, 256, 256)
_IN_NAMES = ("gq", "gk", "gv", "ga", "gg", "cq", "ckv", "kpe", "mg", "dq", "dk", "dv", "dg")


def _rope_perm32():
    p = np.arange(32).reshape(2, 2, 8)[:, ::-1, :].reshape(32)
    return p


def _rope_tables():
    n = T
    row = (np.arange(n) // 64).astype(np.float32)
    col = (np.arange(n) % 64).astype(np.float32)
    half = 16
    inv = (1.0 / (np.float32(10000.0) ** (np.arange(0, half, 2, dtype=np.float32) / np.float32(half)))).astype(np.float32)
    ar = row[:, None] * inv
    ac = col[:, None] * inv
    ang = np.concatenate([ar, ar, ac, ac], axis=-1).astype(np.float32)
    cos, sin = np.cos(ang).astype(np.float32), np.sin(ang).astype(np.float32)
    sign = np.tile(np.concatenate([-np.ones(8), np.ones(8)]), 2).astype(np.float32)
    return cos, sin * sign


_NC_CACHE = {}


def kernel(x_prompt, x_sample, c, cache_mla_ckv, cache_mla_kpe, cache_diff_k, cache_diff_v,
           state_gla, c_ctx, w_ada, b_ada, g_pre, g_post, w_in, w_gla_af, b_gla_af,
           w_gla_ab, b_gla_ab, g_gla, g_mla_q, w_mla_uq, g_mla_kv, w_mla_ukv,
           lam_q1, lam_k1, lam_q2, lam_k2, g_diff, w_out):
    f32 = np.float32
    bf = ml_dtypes.bfloat16
    A = lambda a: np.ascontiguousarray(np.asarray(a, dtype=f32))
    x_prompt, x_sample, c, c_ctx = A(x_prompt), A(x_sample), A(c), A(c_ctx)
    w_in, w_ada, w_out = A(w_in), A(w_ada), A(w_out)
    w_mla_uq, w_mla_ukv = A(w_mla_uq), A(w_mla_ukv)

    offs = np.cumsum((0,) + _IN_SIZES)
    col = {n: (int(offs[i]), int(offs[i + 1])) for i, n in enumerate(_IN_NAMES)}
    perm32 = _rope_perm32()

    def cols(name, a=None, b=None):
        lo, hi = col[name]
        idx = np.arange(lo, hi)
        return idx if a is None else idx[a:b]

    def permuted(idx):
        return idx.reshape(-1, 32)[:, perm32].reshape(-1)

    dq, dk = cols("dq"), cols("dk")
    order = {
        "A1": np.concatenate([cols("gq"), cols("gk")]), "A2": cols("ga"), "A3": cols("gg"),
        "A4": cols("gk"), "A5": cols("gv"),
        "B1": cols("cq"), "B2": cols("mg", 0, 256), "B3": cols("mg", 256, 512),
        "B4": np.concatenate([cols("ckv"), cols("kpe")]),
        "C1": np.concatenate([dq[0:128], permuted(dq[0:128])]),
        "C2": np.concatenate([dq[128:256], permuted(dq[128:256])]),
        "C3": np.concatenate([dk[0:128], permuted(dk[0:128])]),
        "C4": np.concatenate([dk[128:256], permuted(dk[128:256])]),
        "C5": cols("dg"), "C6": cols("dk"), "C7": cols("dv"),
    }
    cidx = np.concatenate([order[g[0]] for g in GROUPS])
    assert cidx.shape[0] == NCOLX
    w_in_x = np.ascontiguousarray(w_in[:, :, cidx])

    uq_idx = np.arange(768).reshape(8, 96)
    uq_cols = np.concatenate([uq_idx, uq_idx[:, 64:96][:, perm32]], axis=1).reshape(-1)
    w_uq_x = np.ascontiguousarray(w_mla_uq[:, :, uq_cols])
    ukv = w_mla_ukv.reshape(2, 128, 8, 128)
    w_ukvk = np.zeros((2, 128, 8, 96), f32)
    w_ukvk[:, :, :, 0:64] = ukv[:, :, :, 0:64]
    w_ukvk = w_ukvk.reshape(2, 128, 768)
    w_ukvv = np.ascontiguousarray(ukv[:, :, :, 64:128]).reshape(2, 128, 512)
    w_gla_x = np.zeros((2, 33, 256), f32)
    w_gla_x[:, 0:16, 0:128] = A(w_gla_af)
    w_gla_x[:, 16:32, 128:256] = A(w_gla_ab)
    w_gla_x[:, 32, 0:128] = A(b_gla_af)
    w_gla_x[:, 32, 128:256] = A(b_gla_ab)
    b_ada = A(b_ada)
    b_ada_c = np.ascontiguousarray(b_ada.reshape(2, 24, 128).transpose(0, 2, 1))
    g_pre_c = np.ascontiguousarray(A(g_pre).reshape(2, 8, 128).transpose(0, 2, 1))
    g_post_c = np.ascontiguousarray(A(g_post).reshape(2, 8, 128).transpose(0, 2, 1))
    g_gla_c = np.ascontiguousarray(np.tile(A(g_gla), (1, 2)).T)
    g_diff_c = np.ascontiguousarray(np.tile(A(g_diff), (1, 2)).T)
    g_mla_q_c = np.ascontiguousarray(A(g_mla_q).reshape(2, 2, 128).transpose(0, 2, 1))
    g_mla_kv_r = A(g_mla_kv).reshape(2, 1, 128)
    lam4 = np.ascontiguousarray(np.stack([A(lam_q1), A(lam_k1), A(lam_q2), A(lam_k2)], axis=1).reshape(2, 1, 128))

    jj = np.arange(128)
    same = (jj[:, None] // 64) == (jj[None, :] // 64)
    le = jj[:, None] <= jj[None, :]
    lt = jj[:, None] < jj[None, :]
    sc16 = f32(-1.0 / 16.0)
    tri = np.zeros((128, 4, 128), f32)
    tri[:, 0, :] = (same & le) * sc16
    tri[:, 1, :] = (same & (~le)) * sc16
    tri[:, 2, :] = (same & (~lt)) * sc16
    tri[:, 3, :] = (same & lt) * sc16
    j64 = (jj % 64)[:, None]
    i64 = np.arange(64)[None, :]
    amask = np.stack([(j64 <= i64), (j64 >= i64)], axis=1).astype(f32)
    ident = np.eye(128, dtype=f32).astype(bf)
    onesbd = np.kron(np.eye(2, dtype=f32), np.ones((64, 64), f32)).astype(bf)
    ipad = np.zeros((32, 96), f32)
    ipad[np.arange(32), 64 + np.arange(32)] = 1.0
    ipad = ipad.astype(bf)
    hmask = (jj[:, None] // 32 == np.arange(4)[None, :]).astype(f32)
    hmf = np.ascontiguousarray(np.broadcast_to((np.arange(4)[:, None] == (jj[None, :] // 32)).astype(f32)[None], (128, 4, 128)))

    cos, sins = _rope_tables()

    def rope_pack(cs, sn):
        rF = np.stack([np.tile(cs.T, (4, 1)), np.tile(sn.T, (4, 1))], axis=1).astype(f32)
        rT = np.stack([cs.reshape(8, 128, 32), sn.reshape(8, 128, 32)], axis=2).transpose(1, 0, 2, 3)
        return np.ascontiguousarray(rF), np.ascontiguousarray(rT.astype(f32))

    ropeF_s, ropeT_s = rope_pack(cos, sins)
    ropeF_p, ropeT_p = rope_pack(np.ones_like(cos), np.zeros_like(sins))

    tq = np.arange(T)
    maskq_p = np.zeros((5, T), f32)
    maskk_p = np.zeros((5, NKEY), f32)
    for b in range(4):
        maskq_p[b] = (tq // 256 == b)
        maskk_p[b, 0:T] = (tq // 256 == b) * BIG
    maskq_p[4] = 1.0
    maskk_p[4] = -BIG
    maskq_s = np.zeros((5, T), f32)
    maskk_s = np.zeros((5, NKEY), f32)

    keepC_s = np.ones((128, 2, 16), f32)
    keepC_p = np.ones((128, 2, 16), f32)
    for cch in range(16):
        if cch % 4 == 0 and cch > 0:
            keepC_p[:, 0, cch] = 0.0
        if (cch + 1) % 4 == 0 and cch < 15:
            keepC_p[:, 1, cch] = 0.0
    def toP(kc):
        kp = kc.copy()
        kp[:, 1, :] = kc[:, 1, ::-1]
        return np.ascontiguousarray(kp)
    keepP_s, keepP_p = toP(keepC_s), toP(keepC_p)

    ipad128 = np.zeros((128, 96), bf)
    ipad128[0:32] = ipad
    cpack_b = np.ascontiguousarray(np.concatenate([ident, onesbd, ipad128], axis=1))
    P128 = lambda a: np.asarray(a, f32).reshape(128, -1)
    bc = lambda a: np.broadcast_to(np.asarray(a, f32).reshape(1, 2, 128), (128, 2, 128))

    def cpack_f(ropeF_, ropeT_, keepC_, keepP_, cvecT_):
        parts = [tri, amask, ropeF_, ropeT_, hmask, hmf, keepC_, keepP_, cvecT_,
                 g_pre_c.transpose(1, 0, 2), b_ada_c.transpose(1, 0, 2), g_post_c.transpose(1, 0, 2),
                 g_gla_c, g_diff_c, g_mla_q_c.transpose(1, 0, 2), bc(g_mla_kv_r), bc(lam4)]
        return np.ascontiguousarray(np.concatenate([P128(p) for p in parts], axis=1))

    shared = dict(w_ada=w_ada, w_in_x=w_in_x, w_uq_x=w_uq_x, w_ukvk=w_ukvk, w_ukvv=w_ukvv, w_out=w_out,
                  w_gla_x=w_gla_x, cpack_b=cpack_b)
    z = lambda *s: np.zeros(s, f32)
    in_maps = []
    for core in range(8):
        d = dict(shared)
        if core < 4:
            b = core
            d.update(x=x_sample[b], cpack_f=cpack_f(ropeF_s, ropeT_s, keepC_s, keepP_s, c[b].reshape(8, 128).T),
                     c_ckv=np.ascontiguousarray(A(cache_mla_ckv[b]).reshape(2, 4, 128, 128).transpose(0, 2, 1, 3)),
                     c_kpe=np.ascontiguousarray(A(cache_mla_kpe[b]).reshape(2, 4, 128, 32).transpose(0, 2, 1, 3)),
                     c_dk=np.ascontiguousarray(A(cache_diff_k[b]).reshape(2, 4, 4, 128, 64).transpose(0, 3, 2, 1, 4).reshape(2, 128, 4, 256)),
                     c_dv=np.ascontiguousarray(A(cache_diff_v[b]).reshape(2, 4, 4, 128, 64).transpose(0, 3, 2, 1, 4).reshape(2, 128, 4, 256)),
                     s_gla=A(state_gla[b]).reshape(2, 2, 128, 64),
                     maskq=maskq_s.astype(bf), maskk=maskk_s.astype(bf))
        else:
            j = core - 4
            d.update(x=np.ascontiguousarray(x_prompt[4 * j:4 * j + 4].reshape(T, D)),
                     cpack_f=cpack_f(ropeF_p, ropeT_p, keepC_p, keepP_p, c_ctx.reshape(8, 128).T),
                     c_ckv=z(2, 128, 4, 128), c_kpe=z(2, 128, 4, 32), c_dk=z(2, 128, 4, 256), c_dv=z(2, 128, 4, 256),
                     s_gla=z(2, 2, 128, 64), maskq=maskq_p.astype(bf), maskk=maskk_p.astype(bf))
        in_maps.append(d)

    if "nc" not in _NC_CACHE:
        _NC_CACHE["nc"] = build_program()
    nc = _NC_CACHE["nc"]
    res = run_bass_kernel_spmd(nc, in_maps, core_ids=list(range(8)))
    R = res.results

    y_sample = np.stack([R[b]["y"] for b in range(4)], axis=0).astype(f32)
    y_prompt = np.concatenate([R[4 + j]["y"].reshape(4, 256, D) for j in range(4)], axis=0).astype(f32)

    def gather(name, inner):
        outs = []
        for j in range(4):
            o = R[4 + j][name]
            outs.append(o.reshape(2, 4, 256, inner).transpose(1, 0, 2, 3))
        return np.concatenate(outs, axis=0)
    new_ckv = np.ascontiguousarray(gather("o_ckv", 128)).astype(f32)
    new_kpe = np.ascontiguousarray(gather("o_kpe", 32)).astype(f32)
    dk_ = gather("o_dk", 256).reshape(16, 2, 256, 4, 64).transpose(0, 1, 3, 2, 4)
    dv_ = gather("o_dv", 256).reshape(16, 2, 256, 4, 64).transpose(0, 1, 3, 2, 4)
    new_dk = np.ascontiguousarray(dk_).astype(f32)
    new_dv = np.ascontiguousarray(dv_).astype(f32)
    gl = []
    for j in range(4):
        o = R[4 + j]["o_gla"]
        gl.append(o.transpose(2, 0, 1, 3, 4).reshape(4, 2, 2, 4, 32, 64))
    new_gla = np.ascontiguousarray(np.concatenate(gl, axis=0)).astype(f32)
    return (y_prompt, y_sample, new_ckv, new_kpe, new_dk, new_dv, new_gla)
```
